# Optimizing a Trainium2 kernel written in Bass

```python
import math
import jax, jax.numpy as jnp
from jax import lax
import numpy as np

D_MODEL = 1024
BATCH = 8
SEQ = 2048
DEPTH = 1

HEAD_DIM = 64
ATTN_GROUPS = ((128, 1), (512, 4), (2048, 16))
HEADS_PER_GROUP = 8
N_HEADS = HEADS_PER_GROUP * len(ATTN_GROUPS)
ATTN_WIDTH = N_HEADS * HEAD_DIM
ATTN_OUT = HEADS_PER_GROUP * HEAD_DIM
POOL_WINDOWS = (2, 4, 8, 16)
POOL_WIDTH = D_MODEL // 2
POOL_GROUP = POOL_WIDTH // len(POOL_WINDOWS)
D_FF = ((8 * D_MODEL // 3 + 255) // 256) * 256
ROPE_THETA = 10000.0
RMS_EPS = 1e-6
Q_BLOCK = 64
NEG_INF = -1e30
IN_SPLITS = (POOL_WIDTH, ATTN_WIDTH, ATTN_WIDTH, ATTN_WIDTH, D_MODEL, D_MODEL)
IN_WIDTH = sum(IN_SPLITS)

kernel_name = "hybrid_pool_dilated_attn_gated_encoder"


def rms_norm(x, g):
    xf = x.astype(jnp.float32)
    y = xf * lax.rsqrt(jnp.mean(xf * xf, axis=-1, keepdims=True) + RMS_EPS)
    return (y * g.astype(jnp.float32)).astype(x.dtype)


def rope_tables(seq):
    pos = jnp.arange(seq, dtype=jnp.float32)
    inv_freq = 1.0 / (ROPE_THETA ** (jnp.arange(0, HEAD_DIM, 2, dtype=jnp.float32) / HEAD_DIM))
    ang = pos[:, None] * inv_freq[None, :]
    ang = jnp.concatenate([ang, ang], axis=-1)
    return jnp.cos(ang), jnp.sin(ang)


def apply_rope(x, cos, sin):
    xf = x.astype(jnp.float32)
    half = HEAD_DIM // 2
    rot = jnp.concatenate([-xf[..., half:], xf[..., :half]], axis=-1)
    out = xf * cos[None, :, None, :] + rot * sin[None, :, None, :]
    return out.astype(x.dtype)


def pool_mixer(u, w_grp, scale):
    B, S, C = u.shape
    uf = u.astype(jnp.float32)
    cs = jnp.concatenate([jnp.zeros((B, 1, C), jnp.float32), jnp.cumsum(uf, axis=1)], axis=1)
    t = jnp.arange(S)
    outs = []
    for gi, w in enumerate(POOL_WINDOWS):
        r = w // 2
        lo = jnp.clip(t - r, 0, S)
        hi = jnp.clip(t + r + 1, 0, S)
        sl = slice(gi * POOL_GROUP, (gi + 1) * POOL_GROUP)
        csg = cs[:, :, sl]
        seg = jnp.take(csg, hi, axis=1) - jnp.take(csg, lo, axis=1)
        cnt = (hi - lo).astype(jnp.float32)
        outs.append(seg / cnt[None, :, None] - uf[:, :, sl])
    grouped = jnp.stack(outs, axis=2)
    mixed = jnp.einsum('bsgc,gcd->bsgd', grouped, w_grp.astype(jnp.float32))
    return (mixed.reshape(B, S, C) * scale.astype(jnp.float32)).astype(u.dtype)


def dilated_window_attention(q, k, v, dilation, radius):
    B, S, H, Dh = q.shape
    L = S // dilation
    blk = math.gcd(L, Q_BLOCK)
    nb = L // blk
    nk = blk + 2 * radius

    def to_classes(a):
        return a.reshape(B, L, dilation, H, Dh).transpose(0, 2, 1, 3, 4)

    qc, kc, vc = to_classes(q), to_classes(k), to_classes(v)
    pad = ((0, 0), (0, 0), (radius, radius), (0, 0), (0, 0))
    kp, vp = jnp.pad(kc, pad), jnp.pad(vc, pad)
    key_idx = jnp.arange(nb)[:, None] * blk + jnp.arange(nk)[None, :]
    kb = jnp.take(kp, key_idx, axis=2)
    vb = jnp.take(vp, key_idx, axis=2)
    qb = qc.reshape(B, dilation, nb, blk, H, Dh)
    s = jnp.einsum('brnqhd,brnkhd->brnhqk', qb, kb).astype(jnp.float32) * (Dh ** -0.5)
    rel = jnp.arange(nk)[None, :] - radius - jnp.arange(blk)[:, None]
    kpos = key_idx - radius
    mask = (jnp.abs(rel) <= radius)[None] & ((kpos >= 0) & (kpos < L))[:, None, :]
    s = jnp.where(mask[None, None, :, None], s, NEG_INF)
    lse = jax.nn.logsumexp(s, axis=-1)
    p = jnp.exp(s - lse[..., None])
    o = jnp.einsum('brnhqk,brnkhd->brnqhd', p.astype(v.dtype), vb)
    o = o.reshape(B, dilation, L, H, Dh).transpose(0, 2, 1, 3, 4).reshape(B, S, H, Dh)
    lse = lse.transpose(0, 1, 2, 4, 3).reshape(B, dilation, L, H).transpose(0, 2, 1, 3).reshape(B, S, H)
    return o, lse


def setup_inputs(seed: int = 0) -> dict:
    key = jax.random.key(seed)
    ks = jax.random.split(key, 16)
    f32 = jnp.float32

    def w(k, shape, fan_in):
        return jax.random.normal(k, shape, f32) * (fan_in ** -0.5)

    def gain(k):
        return 1.0 + 0.05 * jax.random.normal(k, (DEPTH, D_MODEL), f32)

    return {
        "x": jax.random.normal(ks[0], (BATCH, SEQ, D_MODEL), f32),
        "norm_mix_pre": gain(ks[1]),
        "w_in": w(ks[2], (DEPTH, D_MODEL, IN_WIDTH), D_MODEL),
        "w_pool_grp": w(ks[3], (DEPTH, len(POOL_WINDOWS), POOL_GROUP, POOL_GROUP), POOL_GROUP),
        "pool_scale": 1.0 + 0.1 * jax.random.normal(ks[4], (DEPTH, POOL_WIDTH), f32),
        "w_pool_br": w(ks[5], (DEPTH, POOL_WIDTH, D_MODEL), POOL_WIDTH),
        "w_attn_br": w(ks[6], (DEPTH, ATTN_OUT, D_MODEL), ATTN_OUT),
        "w_out": w(ks[7], (DEPTH, D_MODEL, D_MODEL), D_MODEL),
        "norm_mix_post": gain(ks[8]),
        "norm_ffn_pre": gain(ks[9]),
        "w_ffn_gate": w(ks[10], (DEPTH, D_MODEL, D_FF), D_MODEL),
        "w_ffn_up": w(ks[11], (DEPTH, D_MODEL, D_FF), D_MODEL),
        "w_ffn_down": w(ks[12], (DEPTH, D_FF, D_MODEL), D_FF),
        "norm_ffn_post": gain(ks[13]),
    }


def reference(x, norm_mix_pre, w_in, w_pool_grp, pool_scale, w_pool_br, w_attn_br, w_out,
              norm_mix_post, norm_ffn_pre, w_ffn_gate, w_ffn_up, w_ffn_down, norm_ffn_post):
    B, S, _ = x.shape
    cos, sin = rope_tables(S)
    split_pts = list(np.cumsum(IN_SPLITS)[:-1])
    for l in range(DEPTH):
        h = rms_norm(x, norm_mix_pre[l])
        proj = h @ w_in[l]
        u_pool, q, k, v, g_pool, g_attn = jnp.split(proj, split_pts, axis=-1)

        y_pool = pool_mixer(u_pool, w_pool_grp[l], pool_scale[l]) @ w_pool_br[l]

        q = apply_rope(q.reshape(B, S, N_HEADS, HEAD_DIM), cos, sin)
        k = apply_rope(k.reshape(B, S, N_HEADS, HEAD_DIM), cos, sin)
        v = v.reshape(B, S, N_HEADS, HEAD_DIM)
        outs, lses = [], []
        for gi, (window, dil) in enumerate(ATTN_GROUPS):
            hs = slice(gi * HEADS_PER_GROUP, (gi + 1) * HEADS_PER_GROUP)
            radius = (window // 2) // dil
            o, lse = dilated_window_attention(q[:, :, hs], k[:, :, hs], v[:, :, hs], dil, radius)
            outs.append(o)
            lses.append(lse)
        wts = jax.nn.softmax(jnp.stack(lses, axis=0), axis=0)
        o_attn = jnp.sum(wts[..., None].astype(v.dtype) * jnp.stack(outs, axis=0), axis=0)
        y_attn = o_attn.reshape(B, S, ATTN_OUT) @ w_attn_br[l]

        mix = (jax.nn.sigmoid(g_pool) * y_pool + jax.nn.sigmoid(g_attn) * y_attn) @ w_out[l]
        x = x + rms_norm(mix, norm_mix_post[l])

        h2 = rms_norm(x, norm_ffn_pre[l])
        f = (jax.nn.silu(h2 @ w_ffn_gate[l]) * (h2 @ w_ffn_up[l])) @ w_ffn_down[l]
        x = x + rms_norm(f, norm_ffn_post[l])
    return x
```

```python
import numpy as np
from contextlib import ExitStack
import concourse.bass as bass
import concourse.mybir as mybir
from concourse.bass_utils import run_bass_kernel_spmd

F32 = mybir.dt.float32
BF16 = mybir.dt.bfloat16
ALU = mybir.AluOpType
AF = mybir.ActivationFunctionType

S = 2048
D = 1024
NT = 16
DFF = 2816
NF = 22
EPS = 1e-6
POOL_R = (1, 2, 4, 8)
GROUP_DIL = (1, 4, 16)

ARENA_WORDS = 52900
KNOB = dict(pairs=4, groups=3, attn=True, norm=True, njobs=32, sub=15, rope=9, v=0)


_CACHE = {}


class Tracker:
    def __init__(self, nc, es):
        self.nc = nc
        self.es = es
        self.q = {e: [] for e in ("pe", "act", "dve", "pool", "sp")}
        self.sem = {}
        self.cnt = {}
        self.seen = {e: {} for e in self.q}
        self.lastw = {}
        self.readers = {}
        self.log = []
        for e in ("pe", "act", "dve", "pool"):
            self._mk(e)

    def _mk(self, name):
        self.sem[name] = self.es.enter_context(self.nc.semaphore(name))
        self.cnt[name] = 0

    def _deps(self, reads, writes):
        deps = []
        for r in reads:
            t = self.lastw.get(r)
            if t:
                deps.append(t)
        for w in writes:
            t = self.lastw.get(w)
            if t:
                deps.append(t)
            deps.extend(self.readers.get(w, {}).items())
        return deps

    def _waits(self, eng, deps):
        need = {}
        for (s, v) in deps:
            if eng == "pe" and s == "pe":
                continue
            if s == "pe" and v > self.cnt["pe"]:
                raise AssertionError("wait on a future PE mark")
            if self.seen[eng].get(s, 0) < v:
                need[s] = max(need.get(s, 0), v)
        for s, v in need.items():
            self.seen[eng][s] = v
            sem = self.sem[s]
            self.log.append((eng, "wait", s, v))
            self.q[eng].append(lambda e, sem=sem, v=v: e.wait_ge(sem, v))

    def _record(self, tok, reads, writes):
        for r in reads:
            d = self.readers.setdefault(r, {})
            d[tok[0]] = max(d.get(tok[0], 0), tok[1])
        for w in writes:
            self.lastw[w] = tok
            self.readers[w] = {}

    def op(self, eng, fn, reads=(), writes=(), mark=True):
        psr = tuple(r for r in reads if isinstance(r, tuple) and r[0] == "ps")
        if psr:
            writes = tuple(writes) + psr
        self._waits(eng, self._deps(reads, writes))
        if mark:
            self.cnt[eng] += 1
            sem = self.sem[eng]
            self.q[eng].append(lambda e, fn=fn, sem=sem: fn(e).then_inc(sem, 1))
            tok = (eng, self.cnt[eng])
        else:
            self.q[eng].append(lambda e, fn=fn: fn(e))
            tok = (eng, self.cnt[eng] + 1)
        self.log.append((eng, "op", tok, mark))
        self._record(tok, reads, writes)

    def dma(self, qeng, key, fn, reads=(), writes=()):
        if key not in self.sem:
            self._mk(key)
        self._waits(qeng, self._deps(reads, writes))
        self.cnt[key] += 16
        sem = self.sem[key]
        self.q[qeng].append(lambda e, fn=fn, sem=sem: fn(e).then_inc(sem, 16))
        self.log.append((qeng, "dma", (key, self.cnt[key]), True))
        self._record((key, self.cnt[key]), reads, writes)

    def retoken(self, key, resources):
        for r in resources:
            self.lastw[r] = (key, self.cnt[key])

    def barrier(self):
        deps = [(s, c) for s, c in self.cnt.items() if c > 0]
        for eng in self.q:
            self._waits(eng, deps)


def build_program(dbg=None, stop_after=99):
    nc = bass.Bass("TRN2", target_bir_lowering=False)

    def din(name, shape):
        return nc.dram_tensor(name, list(shape), F32, kind="ExternalInput").ap()

    x = din("x", (S, D))
    w_in = din("w_in", (D, 7168))
    w_pgrp = din("w_pgrp", (512, 128))
    pscale = din("pscale", (128, 4))
    w_pbr = din("w_pbr", (512, D))
    w_abr = din("w_abr", (512, D))
    w_out = din("w_out", (D, D))
    gains = din("gains", (4, D))
    w_fg = din("w_fg", (D, DFF))
    w_fu = din("w_fu", (D, DFF))
    w_fd = din("w_fd", (DFF, D))
    c_cs = din("c_cs", (128, 4096))
    c_bf = din("c_bf", (128, 704))
    c_band = din("c_band", (128, 4608))
    c_icnt = din("c_icnt", (128, 1536))
    c_mask = din("c_mask", (128, 1024))
    y = nc.dram_tensor("y", [S, D], F32, kind="ExternalOutput").ap()
    dbg_out = {}
    if dbg:
        for name, shape in dbg.items():
            dbg_out[name] = nc.dram_tensor("dbg_" + name, list(shape), F32, kind="ExternalOutput").ap()

    with ExitStack() as es:
        arena = es.enter_context(nc.sbuf_tensor("arena", [128, ARENA_WORDS], F32))
        ps = es.enter_context(nc.psum_tensor("ps", [128, 4096], F32))
        tk = Tracker(nc, es)
        _CACHE['tk'] = tk

        def f32v(off, n):
            return arena[:, off:off + n]

        def bfv(off, nwords):
            return arena[:, off:off + nwords].bitcast(BF16)

        def bank(k, n=1):
            return ps[:, k * 512:(k + n) * 512]

        def MM(out, lhsT, rhs, start=True, stop=True, reads=(), writes=(), mark=False):
            tk.op("pe", lambda e: e.matmul(out, lhsT=lhsT, rhs=rhs, start=start, stop=stop), reads, writes, mark)

        def TR(out, in_, reads=(), writes=(), mark=False):
            tk.op("pe", lambda e: e.transpose(out, in_, ident), reads, writes, mark)

        def ACT(out, in_, func, reads=(), writes=(), scale=None, accum_out=None):
            kw = {}
            if scale is not None:
                kw["scale"] = scale
            if accum_out is not None:
                kw["accum_out"] = accum_out
            tk.op("act", lambda e: e.activation(out=out, in_=in_, func=func, **kw), reads, writes)

        def TT(out, in0, in1, op, reads=(), writes=(), eng="dve"):
            tk.op(eng, lambda e: e.tensor_tensor(out=out, in0=in0, in1=in1, op=op), reads, writes)

        def STT(out, in0, scalar, in1, op0, op1, reads=(), writes=()):
            tk.op("dve", lambda e: e.scalar_tensor_tensor(out=out, in0=in0, scalar=scalar, in1=in1, op0=op0, op1=op1),
                  reads, writes)

        def TS(out, in0, s1, s2, op0, op1=None, reads=(), writes=()):
            if op1 is None:
                tk.op("dve", lambda e: e.tensor_scalar(out=out, in0=in0, scalar1=s1, scalar2=None, op0=op0), reads, writes)
            else:
                tk.op("dve", lambda e: e.tensor_scalar(out=out, in0=in0, scalar1=s1, scalar2=s2, op0=op0, op1=op1),
                      reads, writes)

        def VCOPY(out, in_, reads=(), writes=()):
            tk.op("dve", lambda e: e.tensor_copy(out=out, in_=in_), reads, writes)

        def RCP(out, in_, reads=(), writes=()):
            tk.op("dve", lambda e: e.reciprocal(out=out, in_=in_), reads, writes)

        def MEMSET(out, val, reads=(), writes=()):
            tk.op("dve", lambda e: e.memset(out, val), reads, writes)

        def DMA(qeng, key, out, in_, reads=(), writes=()):
            tk.dma(qeng, key, lambda e: e.dma_start(out=out, in_=in_), reads, writes)

        dump_n = [0]

        def DUMP(name, src, res):
            if name in dbg_out:
                tk.barrier()
                dst = dbg_out[name]
                DMA("pool", "c0", dst, src, reads=res, writes=("dbg" + name,))
                tk.barrier()

        O_HT = 0
        O_C1 = 8192
        O_SM = 13312
        A = 13824
        hT = bfv(O_HT, 8192).rearrange("p (c t) -> p c t", c=8)
        smbf = bfv(O_SM, 352)
        rotR = smbf[:, 0:128]
        ident = smbf[:, 128:256]
        mask3 = smbf[:, 256:640]
        onesb = smbf[:, 640:704]
        psc = f32v(O_SM + 352, 4)
        stat = f32v(O_SM + 360, 128)
        mixedT = bfv(A + 0, 4096).rearrange("p (g t) -> p g t", g=4)
        oattnT = bfv(A + 4096, 4096).rearrange("p (j t) -> p j t", j=4)
        ostg = bfv(A + 8192, 1024)

        stat_i = [0]

        epsb = stat[:, 120:121]
        MEMSET(epsb, EPS, writes=("epsb",))
        onesf = stat[:, 56:120]
        MEMSET(onesf, 1.0, writes=("onesf",))
        ACT(stat[:, 121:122], epsb, AF.Ln, reads=("epsb",), writes=("warm",))
        ACT(stat[:, 122:123], epsb, AF.Exp, reads=("epsb",), writes=("warm",))

        def rstd_act(src_ap, src_res, junk_ap, junk_res):
            k = stat_i[0] % 14
            stat_i[0] += 1
            st = stat[:, k * 4:(k + 1) * 4]
            r = ("stat", k)
            tk.op("pool", lambda e: e.memset(st[:, 0:1], 0.0), (), (r,))
            ACT(junk_ap, src_ap, AF.Square, reads=src_res + (r,), writes=junk_res + (r,), accum_out=st[:, 0:1])
            tk.op("act", lambda e: e.activation(out=st[:, 1:2], in_=st[:, 0:1], func=AF.Ln, bias=epsb, scale=1.0 / D),
                  (r, "epsb"), (r,))
            ACT(st[:, 2:3], st[:, 1:2], AF.Exp, reads=(r,), writes=(r,), scale=-0.5)
            return st[:, 2:3], r

        g0 = f32v(O_C1, 1024)
        band = bfv(O_C1 + 1024, 2304).rearrange("p (g v e t) -> p g v e t", g=4, v=3, e=3)
        icnt = f32v(O_C1 + 3328, 1536).rearrange("p (g v t) -> p g v t", g=4, v=3)
        DMA("pool", "c0", smbf, c_bf, writes=("smbf",))
        DMA("pool", "c0", band.rearrange("p g v e t -> p (g v e t)"), c_band, writes=("C1b",))
        tk.retoken("c0", ("smbf", "C1b"))

        def late_consts():
            DMA("sp", "c2", g0, gains[0:1, :].partition_broadcast(128), writes=("C1g",))

        def late_consts2():
            DMA("sp", "c1", psc, pscale, writes=("psc",))
            DMA("sp", "c1", f32v(O_C1 + 3328, 1536), c_icnt, writes=("C1i",))
            tk.retoken("c1", ("psc", "C1i"))

        Wu = bfv(A + 12288, 2048).rearrange("p (c f) -> p c f", c=8)
        wgrp = bfv(A + 14336, 256).rearrange("p (g d) -> p g d", g=4)
        DMA("pool", "w6", Wu, w_in[:, 0:512].rearrange("(c p) f -> p c f", p=128), writes=("Wu",))
        DMA("pool", "w7", wgrp, w_pgrp.rearrange("(g c) d -> c g d", c=128), writes=("wgrp",))
        wring = [bfv(A + 34096 + i * 512, 512).rearrange("p (c f) -> p c f", c=8) for i in range(6)]
        units = [(jp, g) for jp in range(4) for g in range(3)]
        wsl = {}

        def issue_weights(u):
            jp, g = units[u]
            fc = g * 4 + jp
            ws = []
            for m, off in enumerate((512, 2048, 3584)):
                s_ = (u % 2) * 3 + m
                DMA("pool", "w%d" % s_, wring[s_],
                    w_in[:, off + fc * 128: off + (fc + 1) * 128].rearrange("(c p) f -> p c f", p=128),
                    writes=(("wr", s_),))
                ws.append(s_)
            wsl[u] = ws


        xring = [f32v(A + 24576 + i * 1024, 1024) for i in range(3)]
        hbf = [bfv(A + 27648 + i * 512, 512) for i in range(3)]
        gb = [0]

        def gbank():
            k = gb[0] % 8
            gb[0] += 1
            return k

        def to_featmajor(i, src_bf, src_res, dstT):
            k = gbank()
            pb = bank(k).bitcast(BF16).rearrange("p (c t) -> p c t", c=8)
            for c in range(8):
                TR(pb[:, c, :], src_bf[:, c * 128:(c + 1) * 128], reads=src_res + ("smbf",), writes=(("ps", k),),
                   mark=(c == 7))
            ACT(dstT[:, :, i * 128:(i + 1) * 128], pb, AF.Copy, reads=(("ps", k),), writes=(("hT", i),))

        u_tok = bfv(A + 16384, 4096).rearrange("p (i f) -> p i f", i=16)
        p1st = {}

        def p1_s1(i):
            sl = i % 3
            DMA("sp", "x%d" % sl, xring[sl], x[i * 128:(i + 1) * 128, :], writes=(("xr", sl),))
            hb = hbf[i % 3]
            p1st[i] = rstd_act(xring[sl], (("xr", sl),), hb, (("hbf", i % 3),))

        def p1_s2(i):
            sl = i % 3
            hb = hbf[i % 3]
            rs, rr = p1st.pop(i)
            STT(hb, xring[sl], rs, g0, ALU.mult, ALU.mult, reads=(("xr", sl), rr, "C1g"), writes=(("hbf", i % 3),))

        def p1_s3(i):
            hb = hbf[i % 3]
            k = i % 4
            pb = bank(k).bitcast(BF16).rearrange("p (c t) -> p c t", c=8)
            for c in range(8):
                TR(pb[:, c, :], hb[:, c * 128:(c + 1) * 128], reads=(("hbf", i % 3), "smbf"), writes=(("ps", k),),
                   mark=(c == 7))

        def p1_s4(i):
            k = i % 4
            pb = bank(k).bitcast(BF16).rearrange("p (c t) -> p c t", c=8)
            if i % 2 == 0:
                ACT(hT[:, :, i * 128:(i + 1) * 128], pb, AF.Copy, reads=(("ps", k),), writes=(("hT", i),))
            else:
                VCOPY(hT[:, :, i * 128:(i + 1) * 128], pb, reads=(("ps", k),), writes=(("hT", i),))

        def p1_s5(i):
            k = 4 + i % 4
            for c in range(8):
                MM(bank(k), hT[:, c, i * 128:(i + 1) * 128], Wu[:, c, :], start=(c == 0), stop=(c == 7),
                   reads=(("hT", i), "Wu"), writes=(("ps", k),), mark=(c == 7))

        def p1_s6(i):
            k = 4 + i % 4
            VCOPY(u_tok[:, i, :], bank(k), reads=(("ps", k),), writes=(("u", i),))

        for t in range(NT + 5):
            for stg, fn in enumerate((p1_s1, p1_s2, p1_s3, p1_s4, p1_s5, p1_s6)):
                if 0 <= t - stg < NT:
                    fn(t - stg)
                if t == 0 and stg == 0:
                    late_consts()
                if t == 2 and stg == 0:
                    late_consts2()
        HT_ALL = tuple(("hT", i) for i in range(NT))
        DUMP("hT", hT[:, 0, :], HT_ALL)

        def _rest():
            grpT = bfv(A + 20480, 4096).rearrange("p (g t) -> p g t", g=4)
            issue_weights(0)
            issue_weights(1)
            for g in range(4):
                for tb in range(4):
                    k = gbank()
                    for j in range(4):
                        i = 4 * tb + j
                        var = 0 if i == 0 else (2 if i == NT - 1 else 1)
                        es_ = [e for e in (-1, 0, 1) if 0 <= i + e < NT]
                        for n_, e in enumerate(es_):
                            MM(bank(k)[:, j * 128:(j + 1) * 128], u_tok[:, i + e, g * 128:(g + 1) * 128],
                               band[:, g, var, e + 1, :], start=(n_ == 0), stop=(n_ == len(es_) - 1),
                               reads=(("u", i + e), "C1b"), writes=(("ps", k),), mark=(j == 3 and n_ == len(es_) - 1))
                    inv = 1.0 / (2 * POOL_R[g] + 1)
                    runs = {0: [(1, 4)], 3: [(0, 3)]}.get(tb, [(0, 4)])
                    for (j0, j1) in runs:
                        i0_ = 4 * tb + j0
                        i1_ = 4 * tb + j1
                        tk.op("act", lambda e, o_=grpT[:, g, i0_ * 128:i1_ * 128], i_=bank(k)[:, j0 * 128:j1 * 128],
                              m_=inv: e.mul(out=o_, in_=i_, mul=m_),
                              (("ps", k),), tuple(("grp", g, ii) for ii in range(i0_, i1_)))
                    for j in range(4):
                        i = 4 * tb + j
                        if i not in (0, NT - 1):
                            continue
                        var = 0 if i == 0 else 2
                        TT(grpT[:, g, i * 128:(i + 1) * 128], bank(k)[:, j * 128:(j + 1) * 128], icnt[:, g, var, :], ALU.mult,
                           reads=(("ps", k), "C1i"), writes=(("grp", g, i),))
            for g in range(4):
                for tb in range(4):
                    k = gbank()
                    MM(bank(k), wgrp[:, g, :], grpT[:, g, tb * 512:(tb + 1) * 512],
                       reads=tuple(("grp", g, 4 * tb + j) for j in range(4)) + ("wgrp",), writes=(("ps", k),), mark=True)
                    TS(mixedT[:, g, tb * 512:(tb + 1) * 512], bank(k), psc[:, g:g + 1], None, ALU.mult,
                       reads=(("ps", k), "psc"), writes=(("mixT", g, tb),))
            DUMP("mixedT", mixedT[:, 0, :], tuple(("mixT", 0, tb) for tb in range(4)))

            yield
            tk.barrier()
            cosT = f32v(O_C1, 2048)
            sinT = f32v(O_C1 + 2048, 2048)
            DMA("sp", "c1", f32v(O_C1, 4096), c_cs, writes=("C1cs",))
            QT = [bfv(A + 12288 + g * 1024, 1024) for g in range(3)]
            KT = [bfv(A + 15360 + g * 1024, 1024) for g in range(3)]
            VTc = [bfv(A + 18432 + i * 1024, 1024) for i in range(2)]
            Vp = [bfv(A + 20480 + g * 1040, 1040).rearrange("p (n h d) -> p n h d", n=16, h=2) for g in range(3)]
            Osum = [f32v(A + 23600 + i * 2048, 2048) for i in range(2)]
            maskG = bfv(A + 9216, 512)
            PT = [bfv(A + 9728 + i * 256, 256) for i in range(4)]
            DMA("pool", "c0", maskG, c_mask, writes=("maskG",))
            abf = [bfv(A + 28464 + i * 256, 256) for i in range(2)]
            t12 = [f32v(A + 28976 + i * 512, 512) for i in range(6)]
            Wab = bfv(A + 32048, 2048).rearrange("p (j d) -> p j d", j=4)
            for g in range(3):
                MEMSET(Vp[g][:, :, :, 64:65], 1.0, writes=(("Vp", g),))

            def cm_view(buf, g, tb):
                d = GROUP_DIL[g]
                if d == 1:
                    return buf[:, tb * 512:(tb + 1) * 512]
                w = 512 // d
                return buf.rearrange("p (r l) -> p r l", r=d)[:, :, tb * w:(tb + 1) * w]

            def nat_src(ap, g):
                d = GROUP_DIL[g]
                if d == 1:
                    return ap
                return ap.rearrange("p (a r) -> p r a", r=d)

            st_i = [0]
            ot_i = [0]
            gen3 = [0]
            pt_i = [0]
            ab_i = [0]
            t_i = [0]
            vt_i = [0]

            def g3bank():
                k = 5 + gen3[0] % 2
                gen3[0] += 1
                return k

            G2 = 7

            def proj_steps(u):
                jp, g = units[u]
                ws = wsl[u]
                pending = [None]
                vt = vt_i[0] % 2
                vt_i[0] += 1

                def proj_mm(k, s_, tb):
                    for c in range(8):
                        MM(bank(k), wring[s_][:, c, :], hT[:, c, tb * 512:(tb + 1) * 512], start=(c == 0),
                           stop=(c == 7), reads=(("wr", s_),) + HT_ALL[4 * tb:4 * tb + 4], writes=(("ps", k),),
                           mark=(c == 7))

                def rope_finish():
                    if pending[0] is None:
                        return
                    ta, a_, dstbuf, nm, tb = pending[0]
                    pending[0] = None
                    k2 = G2
                    MM(bank(k2), rotR, abf[a_], reads=(("abf", a_), "smbf"), writes=(("ps", k2),), mark=True)
                    tb2 = t_i[0] % 6
                    t_i[0] += 1
                    TT(t12[tb2], bank(k2), sinT[:, tb * 512:(tb + 1) * 512], ALU.mult,
                       reads=(("ps", k2), "C1cs"), writes=(("t12", tb2),))
                    TT(cm_view(dstbuf, g, tb), nat_src(t12[ta], g), nat_src(t12[tb2], g), ALU.add,
                       reads=(("t12", ta), ("t12", tb2)), writes=((nm, g),), eng="dve")

                def qk_block(m, dstbuf, nm, tb):
                    st = {}

                    def pe():
                        st["k"] = g3bank()
                        proj_mm(st["k"], ws[m], tb)

                    def post():
                        k = st["k"]
                        a_ = ab_i[0] % 2
                        ab_i[0] += 1
                        ACT(abf[a_], bank(k), AF.Copy, reads=(("ps", k),), writes=(("abf", a_),))
                        ta = t_i[0] % 6
                        t_i[0] += 1
                        TT(t12[ta], bank(k), cosT[:, tb * 512:(tb + 1) * 512], ALU.mult,
                           reads=(("ps", k), "C1cs"), writes=(("t12", ta),))
                        rope_finish()
                        pending[0] = (ta, a_, dstbuf, nm, tb)
                    return pe, post

                def v_block(tb):
                    st = {}

                    def pe():
                        st["k"] = g3bank()
                        proj_mm(st["k"], ws[2], tb)

                    def post():
                        k = st["k"]
                        rope_finish()
                        ACT(cm_view(VTc[vt], g, tb), nat_src(bank(k), g), AF.Copy, reads=(("ps", k),),
                            writes=(("VTc", vt),))
                    return pe, post

                def vtr(half):
                    st = {}

                    def pe():
                        k = st["k"] = g3bank()
                        pb = bank(k).bitcast(BF16).rearrange("p (n f) -> p n f", n=8)
                        for j in range(8):
                            n = half * 8 + j
                            TR(pb[:, j, :], VTc[vt][:, n * 128:(n + 1) * 128], reads=(("VTc", vt), "smbf"),
                               writes=(("ps", k),), mark=(j == 7))

                    def post():
                        k = st["k"]
                        pb = bank(k).bitcast(BF16).rearrange("p (n f) -> p n f", n=8)
                        ACT(Vp[g][:, half * 8:(half + 1) * 8, :, 0:64], pb.rearrange("p n (h d) -> p n h d", h=2),
                            AF.Copy, reads=(("ps", k),), writes=(("Vp", g),))
                    return pe, post

                steps = []
                for tb in range(4):
                    steps.append(qk_block(0, QT[g], "QT", tb))
                for tb in range(4):
                    steps.append(qk_block(1, KT[g], "KT", tb))
                for tb in range(4):
                    steps.append(v_block(tb))
                steps.append(vtr(0))
                steps.append(vtr(1))
                return steps

            def att_steps(u):
                jp, g = units[u]
                d = GROUP_DIL[g]
                T = NT // d
                jobs = [(hh, sb, qt) for hh in range(2) for sb in range(4) for qt in range(4)]
                if T == 1:
                    GT, W, moff = 2, 128, 512
                    COLS = {0: (0, 128)}
                else:
                    GT, W, moff = 2, 256, 0
                    COLS = {-1: (0, 64), 0: (64, 192), 1: (192, 256)}
                QSUB = {-1: (0, 64), 0: (0, 128), 1: (64, 128)}
                groups = [list(range(a, a + GT)) for a in range(0, len(jobs), GT)]
                stres = {}
                otk = [None]

                def tile_es(ji):
                    hh, sb, qt = jobs[ji]
                    n = 4 * sb + qt
                    qc = n % T
                    return n, [e for e in (-1, 0, 1) if 0 <= qc + e < T]

                def emit_st(gi):
                    k = st_i[0] % 3
                    st_i[0] += 1
                    MM(bank(k)[:, 0:GT * W], ident, maskG[:, moff:moff + GT * W], start=True, stop=False,
                       reads=("smbf", "maskG"), writes=(("ps", k),))
                    todo = []
                    for m, ji in enumerate(groups[gi]):
                        hh, sb, qt = jobs[ji]
                        n, es_ = tile_es(ji)
                        pr = slice(hh * 64, (hh + 1) * 64)
                        for e in es_:
                            todo.append((m, n, e, pr))
                    for n_, (m, n, e, pr) in enumerate(todo):
                        a0, a1 = COLS[e]
                        q0, q1 = QSUB[e]
                        MM(bank(k)[:, m * W + a0:m * W + a1], KT[g][pr, (n + e) * 128:(n + e + 1) * 128],
                           QT[g][pr, n * 128 + q0:n * 128 + q1], start=False, stop=(n_ == len(todo) - 1),
                           reads=(("QT", g), ("KT", g)), writes=(("ps", k),), mark=(n_ == len(todo) - 1))
                    stres[gi] = k

                sts = {}

                def pv(gi):
                    p_ = sts.pop(gi)
                    for m, ji in enumerate(groups[gi]):
                        hh, sb, qt = jobs[ji]
                        n, es_ = tile_es(ji)
                        if qt == 0:
                            otk[0] = 3 + ot_i[0] % 2
                            ot_i[0] += 1
                        ok = otk[0]
                        order = [0] + [e for e in es_ if e != 0]
                        for n_, e in enumerate(order):
                            a0, a1 = COLS[e]
                            q0, q1 = QSUB[e]
                            MM(bank(ok)[0:65, qt * 128 + q0:qt * 128 + q1], Vp[g][:, n + e, hh, :],
                               PT[p_][:, m * W + a0:m * W + a1], start=(n_ == 0), stop=(n_ == len(order) - 1),
                               reads=(("PT", p_), ("Vp", g)), writes=(("ps", ok),), mark=(n_ == len(order) - 1))
                        if qt == 3:
                            osl = Osum[hh]
                            src = bank(ok)[0:65, :]
                            if d == 1:
                                dst = osl[0:65, sb * 512:(sb + 1) * 512]
                            elif d == 4:
                                dst = osl[0:65, :].rearrange("p (l r) -> p r l", r=4)[:, sb, :]
                            else:
                                dst = osl[0:65, :].rearrange("p (l r) -> p r l", r=16)[:, 4 * sb:4 * sb + 4, :]
                                src = src.rearrange("p (r l) -> p r l", r=4)
                            if g == 0:
                                ACT(dst, src, AF.Copy, reads=(("ps", ok),), writes=(("Osum", hh),))
                            else:
                                TT(dst, src, dst, ALU.add, reads=(("ps", ok), ("Osum", hh)), writes=(("Osum", hh),))

                NG = len(groups)

                def mk(gi):
                    def head():
                        if gi == 0:
                            for q_ in range(min(3, NG)):
                                emit_st(q_)
                        k = stres[gi]
                        p_ = pt_i[0] % 4
                        pt_i[0] += 1
                        sts[gi] = p_
                        ACT(PT[p_][:, 0:GT * W], bank(k)[:, 0:GT * W], AF.Exp, reads=(("ps", k),), writes=(("PT", p_),),
                            scale=0.125)

                    def tail():
                        if gi >= 1:
                            pv(gi - 1)
                        if gi + 3 < NG:
                            emit_st(gi + 3)
                        if gi == NG - 1:
                            pv(gi)
                    return head, tail

                return [mk(gi) for gi in range(NG)]

            rtmp = [f32v(A + 10752 + i * 512, 512) for i in range(2)]
            rt_i = [0]

            def norm_steps(jp, hh, banks=(7,)):
                osl = Osum[hh]
                ro = ("Osum", hh)

                def blk(tb):
                    def f():
                        k = banks[(hh * 4 + tb) % len(banks)]
                        MM(bank(k)[0:64, :], onesf[64:65, :], osl[64:65, tb * 512:(tb + 1) * 512], start=True, stop=True,
                           reads=(ro, "onesf"), writes=(("ps", k),), mark=True)
                        r_ = rt_i[0] % 2
                        rt_i[0] += 1
                        ACT(rtmp[r_][0:64, :], bank(k)[0:64, :], AF.Ln, reads=(("ps", k),), writes=(("rtmp", r_),))
                        ACT(rtmp[r_][0:64, :], rtmp[r_][0:64, :], AF.Exp, reads=(("rtmp", r_),), writes=(("rtmp", r_),),
                            scale=-1.0)
                        if hh == 0:
                            TT(oattnT[0:64, jp, tb * 512:(tb + 1) * 512], rtmp[r_][0:64, :],
                               osl[0:64, tb * 512:(tb + 1) * 512], ALU.mult, reads=(("rtmp", r_), ro),
                               writes=(("oat", jp, 0),))
                        else:
                            TT(ostg[0:64, tb * 512:(tb + 1) * 512], rtmp[r_][0:64, :],
                               osl[0:64, tb * 512:(tb + 1) * 512], ALU.mult, reads=(("rtmp", r_), ro), writes=("ostg",))
                            if tb == 3:
                                DMA("sp", "os", oattnT[64:128, jp, :], ostg[0:64, :], reads=("ostg",),
                                    writes=(("oat", jp, 1),))
                    return f
                return [blk(tb) for tb in range(4)]

            NU = len(units)
            gpre = [bfv(O_C1 + 4096 + i * 512, 512).rearrange("p (c f) -> p c f", c=8) for i in range(2)]
            for pe_, post_ in proj_steps(0):
                pe_()
                post_()
            for u in range(NU):
                jp, g = units[u]
                Asteps = att_steps(u)
                Psteps = proj_steps(u + 1) if u + 1 < NU else []
                if u + 2 < NU:
                    issue_weights(u + 2)
                if u == NU - 3:
                    DMA("pool", "w7", Wab, w_abr.rearrange("(j p) d -> p j d", p=128), writes=("Wab",))
                if u == NU - 2:
                    for i_, off in enumerate((5120, 6144)):
                        DMA("pool", "gk%d" % i_, gpre[i_],
                            w_in[:, off:off + 128].rearrange("(c p) f -> p c f", p=128), writes=(("gr", 6 + i_),))
                nA, nP = len(Asteps), len(Psteps)
                pos = {}
                GRP = KNOB.get("pgrp", 1)
                ngrp = (nP + GRP - 1) // GRP
                for j in range(nP):
                    pos.setdefault(int((j // GRP + 0.5) * nA / max(ngrp, 1)), []).append(j)
                postq = []
                extra = {}
                if g == 0 and jp >= 1:
                    n0 = norm_steps(jp - 1, 0)
                    n1 = norm_steps(jp - 1, 1)
                    for ii, f_ in zip((0, 1, 1, 2), n0):
                        extra.setdefault(ii, []).append(f_)
                    for ii, f_ in zip((3, 4, 5, 6), n1):
                        extra.setdefault(ii, []).append(f_)
                for i, (head, tail) in enumerate(Asteps):
                    head()
                    for f_ in extra.get(i, ()):
                        f_()
                    for _, p_ in postq:
                        p_()
                    postq = []
                    for j in pos.get(i, ()):
                        keep = []
                        for jj, p_ in postq:
                            if jj <= j - 2:
                                p_()
                            else:
                                keep.append((jj, p_))
                        postq = keep
                        Psteps[j][0]()
                        postq.append((j, Psteps[j][1]))
                    tail()
                for _, p_ in postq:
                    p_()
                if u == NU - 1:
                    for hh_ in range(2):
                        for s_ in norm_steps(jp, hh_, banks=(5, 6, 7)):
                            s_()
            DUMP("QT0", QT[0], (("QT", 0),))
            DUMP("oattnT", oattnT[0:64, 0, :], (("oat", 0, 0),))

            yield
            tk.barrier()
            msumT = bfv(A + 12288, 8192).rearrange("p (c t) -> p c t", c=8)
            Wpb = bfv(A + 20480, 2048).rearrange("p (g d) -> p g d", g=4)
            gring = [bfv(A + 26624 + i * 512, 512).rearrange("p (c f) -> p c f", c=8) for i in range(6)] + gpre
            sg = [f32v(A + 29696 + i * 512, 512) for i in range(4)]
            tt_ = [f32v(A + 22528 + i * 512, 512) for i in range(4)]
            for j in range(3):
                DMA("sp", "c1", f32v(O_C1 + j * 1024, 1024), gains[j + 1:j + 2, :].partition_broadcast(128), writes=(("C1g", j),))
            tk.retoken("c1", tuple(("C1g", j) for j in range(3)))
            Wout = bfv(A + 34096, 4096).rearrange("p (c d) -> p c d", c=8)
            gs_i = [0]

            def issue_gates(dc):
                gsl = []
                for off in (5120, 6144):
                    s_ = gs_i[0] % 6
                    gs_i[0] += 1
                    DMA("pool", "w%d" % s_, gring[s_],
                        w_in[:, off + dc * 128: off + (dc + 1) * 128].rearrange("(c p) f -> p c f", p=128),
                        writes=(("gr", s_),))
                    gsl.append(s_)
                return gsl

            gsl_next = [6, 7]
            DMA("pool", "w6", Wpb, w_pbr.rearrange("(g c) d -> c g d", c=128), writes=("Wpb",))
            sg_i = [0]
            for dc in range(8):
                if dc == 1:
                    DMA("pool", "w6", Wout, w_out.rearrange("(c p) d -> p c d", p=128), reads=("Wpb",), writes=("Wout",))
                gsl = gsl_next
                if dc + 1 < 8:
                    gsl_next = issue_gates(dc + 1)
                for tb in range(4):
                    tsl = slice(tb * 512, (tb + 1) * 512)
                    kA, kB, kC, kD = gbank(), gbank(), gbank(), gbank()
                    for kk, s_ in ((kC, gsl[0]), (kD, gsl[1])):
                        for c in range(8):
                            MM(bank(kk), gring[s_][:, c, :], hT[:, c, tsl], start=(c == 0), stop=(c == 7),
                               reads=(("gr", s_),) + HT_ALL[4 * tb:4 * tb + 4], writes=(("ps", kk),), mark=(c == 7))
                    for j in range(4):
                        MM(bank(kB), Wab[:, j, dc * 128:(dc + 1) * 128], oattnT[:, j, tsl], start=(j == 0),
                           stop=(j == 3), reads=("Wab", ("oat", j, 0), ("oat", j, 1)), writes=(("ps", kB),),
                           mark=(j == 3))
                    for g in range(4):
                        MM(bank(kA), Wpb[:, g, dc * 128:(dc + 1) * 128], mixedT[:, g, tsl], start=(g == 0), stop=(g == 3),
                           reads=("Wpb", ("mixT", g, tb)), writes=(("ps", kA),), mark=(g == 3))
                    s0 = sg_i[0] % 4
                    s1 = (sg_i[0] + 1) % 4
                    sg_i[0] += 2
                    ACT(sg[s0], bank(kC), AF.Sigmoid, reads=(("ps", kC),), writes=(("sg", s0),))
                    ACT(sg[s1], bank(kD), AF.Sigmoid, reads=(("ps", kD),), writes=(("sg", s1),))
                    TT(tt_[s0], bank(kA), sg[s0], ALU.mult, reads=(("ps", kA), ("sg", s0)), writes=(("tt", s0),))
                    TT(tt_[s1], bank(kB), sg[s1], ALU.mult, reads=(("ps", kB), ("sg", s1)), writes=(("tt", s1),))
                    TT(msumT[:, dc, tsl], tt_[s0], tt_[s1], ALU.add, reads=(("tt", s0), ("tt", s1)), writes=(("ms", dc, tb),))
            DUMP("msumT", msumT[:, 0, :], tuple(("ms", 0, tb) for tb in range(4)))

            yield
            tk.barrier()
            xring2 = [f32v(A + 20480 + i * 1024, 1024) for i in range(3)]
            tileA = [f32v(A + 23552 + i * 1024, 1024) for i in range(2)]
            tileB = [f32v(A + 25600 + i * 1024, 1024) for i in range(4)]
            hbf2 = [bfv(A + 29696 + i * 512, 512) for i in range(3)]
            g1 = f32v(O_C1, 1024)
            g2 = f32v(O_C1 + 1024, 1024)
            g3 = f32v(O_C1 + 2048, 1024)
            h2T = hT
            big_i = [0]
            tl_i = [0]
            p4 = {}

            def p4_s1(i):
                sl = i % 3
                DMA("sp", "x%d" % sl, xring2[sl], x[i * 128:(i + 1) * 128, :], writes=(("xr", sl),))
                kb = (i % 3) * 2
                for half in range(2):
                    for c in range(8):
                        MM(bank(kb + half), msumT[:, c, i * 128:(i + 1) * 128], Wout[:, c, half * 512:(half + 1) * 512],
                           start=(c == 0), stop=(c == 7), reads=(("ms", c, i // 4), "Wout"),
                           writes=(("ps", kb + half),), mark=(c == 7))
                ta = i % 2
                pres = (("ps", kb), ("ps", kb + 1))
                rs, rr = rstd_act(bank(kb, 2), pres, tileA[ta], (("tA", ta),))
                p4[i] = dict(rs=rs, rr=rr, kb=kb, ta=ta, tb=i % 4, sl=sl)

            def p4_s2(i):
                d_ = p4[i]
                kb, ta, tb_, sl = d_["kb"], d_["ta"], d_["tb"], d_["sl"]
                pres = (("ps", kb), ("ps", kb + 1))
                STT(tileA[ta], bank(kb, 2), d_["rs"], g1, ALU.mult, ALU.mult, reads=pres + (d_["rr"], ("C1g", 0)),
                    writes=(("tA", ta),))
                TT(tileB[tb_], tileA[ta], xring2[sl], ALU.add, reads=(("tA", ta), ("xr", sl)), writes=(("tB", tb_),))
                DMA("sp", "ys%d" % (i % 4), y[i * 128:(i + 1) * 128, :], tileB[tb_], reads=(("tB", tb_),),
                    writes=(("y", i),))

            def p4_s3(i):
                d_ = p4[i]
                tb_ = d_["tb"]
                hb = hbf2[i % 3]
                d_["rs2"], d_["rr2"] = rstd_act(tileB[tb_], (("tB", tb_),), hb, (("hbf", i % 3),))

            def p4_s4(i):
                d_ = p4[i]
                tb_ = d_["tb"]
                hb = hbf2[i % 3]
                STT(hb, tileB[tb_], d_["rs2"], g2, ALU.mult, ALU.mult, reads=(("tB", tb_), d_["rr2"], ("C1g", 1)),
                    writes=(("hbf", i % 3),))

            def p4_s5(i):
                hb = hbf2[i % 3]
                k = 6 + i % 2
                pb = bank(k).bitcast(BF16).rearrange("p (c t) -> p c t", c=8)
                for c in range(8):
                    TR(pb[:, c, :], hb[:, c * 128:(c + 1) * 128], reads=(("hbf", i % 3), "smbf"), writes=(("ps", k),),
                       mark=(c == 7))

            def p4_s6(i):
                p4.pop(i)
                k = 6 + i % 2
                pb = bank(k).bitcast(BF16).rearrange("p (c t) -> p c t", c=8)
                ACT(h2T[:, :, i * 128:(i + 1) * 128], pb, AF.Copy, reads=(("ps", k),), writes=(("hT", i),))

            for t in range(NT + 5):
                for stg, fn in enumerate((p4_s1, p4_s2, p4_s3, p4_s4, p4_s5, p4_s6)):
                    if 0 <= t - stg < NT:
                        fn(t - stg)
            fpre = [bfv(O_C1 + 3072 + i * 512, 512).rearrange("p (c f) -> p c f", c=8) for i in range(4)]
            for f in range(2):
                for i_, wsrc in enumerate((w_fg, w_fu)):
                    DMA("pool", "w%d" % (2 * f + i_), fpre[2 * f + i_],
                        wsrc[:, f * 128:(f + 1) * 128].rearrange("(c p) f -> p c f", p=128),
                        writes=(("fr", 6 + 2 * f + i_),))
            DUMP("h2T", h2T[:, 0, :], HT_ALL)

            yield
            tk.barrier()
            hidT = bfv(A + 0, 22528).rearrange("p (f t) -> p f t", f=NF)
            fring = [bfv(A + 22528 + i * 512, 512).rearrange("p (c f) -> p c f", c=8) for i in range(6)] + fpre
            sring = [f32v(A + 26624 + i * 512, 512) for i in range(2)]
            Wd = bfv(A + 27648, 11264).rearrange("p (f d) -> p f d", f=NF)
            fs_i = [0]
            sr_i = [0]
            for f in range(NF):
                if f == 2:
                    DMA("pool", "w7", Wd, w_fd.rearrange("(f p) d -> p f d", p=128), writes=("Wd",))
                fsl = []
                for i_, wsrc in enumerate((w_fg, w_fu)):
                    if f < 2:
                        fsl.append(6 + 2 * f + i_)
                        continue
                    s_ = fs_i[0] % 6
                    fs_i[0] += 1
                    DMA("pool", "w%d" % s_, fring[s_],
                        wsrc[:, f * 128:(f + 1) * 128].rearrange("(c p) f -> p c f", p=128), writes=(("fr", s_),))
                    fsl.append(s_)
                for tb in range(4):
                    tsl = slice(tb * 512, (tb + 1) * 512)
                    kA, kB = gbank(), gbank()
                    for kk, s_ in ((kA, fsl[0]), (kB, fsl[1])):
                        for c in range(8):
                            MM(bank(kk), fring[s_][:, c, :], h2T[:, c, tsl], start=(c == 0), stop=(c == 7),
                               reads=(("fr", s_),) + HT_ALL[4 * tb:4 * tb + 4], writes=(("ps", kk),), mark=(c == 7))
                    s0 = sr_i[0] % 2
                    sr_i[0] += 1
                    ACT(sring[s0], bank(kA), AF.Silu, reads=(("ps", kA),), writes=(("sr", s0),))
                    TT(hidT[:, f, tsl], bank(kB), sring[s0], ALU.mult, reads=(("ps", kB), ("sr", s0)), writes=(("hid", f, tb),))
            DUMP("hidT", hidT[:, 0, :], tuple(("hid", 0, tb) for tb in range(4)))

            yield
            tk.barrier()
            xring3 = [f32v(O_HT + i * 1024, 1024) for i in range(3)]
            tile3 = [f32v(O_HT + 3072 + i * 1024, 1024) for i in range(4)]
            p5 = {}

            def p5_s1(i):
                sl = i % 3
                DMA("sp", "x%d" % sl, xring3[sl], y[i * 128:(i + 1) * 128, :], reads=(("y", i),), writes=(("xr", sl),))
                kb = (i % 4) * 2
                for half in range(2):
                    for f in range(NF):
                        MM(bank(kb + half), hidT[:, f, i * 128:(i + 1) * 128], Wd[:, f, half * 512:(half + 1) * 512],
                           start=(f == 0), stop=(f == NF - 1), reads=(("hid", f, i // 4), "Wd"),
                           writes=(("ps", kb + half),), mark=(f == NF - 1))
                ta = (2 * i) % 4
                pres = (("ps", kb), ("ps", kb + 1))
                rs, rr = rstd_act(bank(kb, 2), pres, tile3[ta], (("tile", ta),))
                p5[i] = dict(rs=rs, rr=rr, kb=kb, ta=ta, tb=(2 * i + 1) % 4, sl=sl)

            def p5_s2(i):
                d_ = p5.pop(i)
                kb, ta, tb_, sl = d_["kb"], d_["ta"], d_["tb"], d_["sl"]
                pres = (("ps", kb), ("ps", kb + 1))
                STT(tile3[ta], bank(kb, 2), d_["rs"], g3, ALU.mult, ALU.mult, reads=pres + (d_["rr"], ("C1g", 2)),
                    writes=(("tile", ta),))
                TT(tile3[tb_], tile3[ta], xring3[sl], ALU.add, reads=(("tile", ta), ("xr", sl)), writes=(("tile", tb_),))
                DMA("sp", "ys%d" % (i % 2), y[i * 128:(i + 1) * 128, :], tile3[tb_], reads=(("tile", tb_),),
                    writes=(("y", i),))

            for t in range(NT + 1):
                if t < NT:
                    p5_s1(t)
                if t >= 1:
                    p5_s2(t - 1)

            yield

        _g = _rest()
        _n = 1
        while _n < stop_after:
            try:
                next(_g)
            except StopIteration:
                break
            _n += 1
        tk.barrier()

        block = es.enter_context(nc.Block())

        @block.tensor
        def _(e):
            for f_ in tk.q["pe"]:
                f_(e)

        @block.scalar
        def _(e):
            for f_ in tk.q["act"]:
                f_(e)

        @block.vector
        def _(e):
            for f_ in tk.q["dve"]:
                f_(e)

        @block.gpsimd
        def _(e):
            for f_ in tk.q["pool"]:
                f_(e)

        @block.sync
        def _(e):
            for f_ in tk.q["sp"]:
                f_(e)

    return nc


def _constants():
    f32 = np.float32
    pos = np.arange(S, dtype=f32)
    pw = (10000.0 ** (np.arange(0, 64, 2, dtype=np.float64) / 64.0)).astype(f32)
    inv_freq = (f32(1.0) / pw).astype(f32)
    ang = (pos[:, None] * inv_freq[None, :]).astype(f32)
    cos = np.cos(ang).astype(f32)
    sin = np.sin(ang).astype(f32)
    p = np.arange(128)
    j = (p % 64) % 32
    c_cs = np.concatenate([cos[:, j].T, sin[:, j].T], axis=1).astype(f32)
    ident = np.eye(128, dtype=f32)
    rot = np.zeros((128, 128), f32)
    for m in range(128):
        if m % 64 < 32:
            rot[m + 32, m] = -1.0
        else:
            rot[m - 32, m] = 1.0
    i_ = np.arange(128)[:, None]
    c_ = np.arange(128)[None, :]
    mask3 = np.stack([(np.abs(128 * e + i_ - c_) <= 64).astype(f32) for e in (-1, 0, 1)], axis=1)
    ones = np.ones((128, 64), f32)
    negmask = (mask3 - 1.0) * 30000.0
    negmask = np.concatenate([negmask[:, 0, 0:64], negmask[:, 1, :], negmask[:, 2, 64:128], np.zeros((128, 128), f32)], axis=1)
    c_bf = np.concatenate([rot, ident, negmask, ones], axis=1)
    band = np.zeros((128, 4, 3, 3, 128), f32)
    icnt = np.zeros((128, 4, 3, 128), f32)
    for g, r in enumerate(POOL_R):
        for v, i in enumerate((0, 1, NT - 1)):
            T = 128 * i + np.arange(128)
            cnt = np.minimum(T + r + 1, S) - np.maximum(T - r, 0)
            icnt[:, g, v, :] = (1.0 / cnt.astype(f32))[None, :]
            for ei, e in enumerate((-1, 0, 1)):
                m = (np.abs(128 * e + i_ - c_) <= r).astype(f32)
                if e == 0:
                    m = m - np.diag(cnt.astype(f32))
                band[:, g, v, ei, :] = m
    nm256 = negmask[:, 0:256]
    e0 = negmask[:, 64:192]
    c_mask = np.concatenate([nm256, nm256, e0, e0, e0, e0], axis=1).astype(f32)
    return dict(c_cs=c_cs, c_bf=c_bf.astype(f32), c_band=band.reshape(128, -1), c_icnt=icnt.reshape(128, -1),
                c_mask=c_mask)


def _get_nc():
    if "nc" not in _CACHE:
        _CACHE["nc"] = build_program()
    return _CACHE["nc"]


def make_in_maps(inputs, n_cores=8):
    f = lambda a: np.ascontiguousarray(np.asarray(a, dtype=np.float32))
    consts = _constants()
    shared = dict(
        w_in=f(inputs["w_in"][0]),
        w_pgrp=f(inputs["w_pool_grp"][0]).reshape(512, 128),
        pscale=f(np.asarray(inputs["pool_scale"][0]).reshape(4, 128).T),
        w_pbr=f(inputs["w_pool_br"][0]),
        w_abr=f(inputs["w_attn_br"][0]),
        w_out=f(inputs["w_out"][0]),
        gains=f(np.stack([np.asarray(inputs["norm_mix_pre"][0]), np.asarray(inputs["norm_mix_post"][0]),
                          np.asarray(inputs["norm_ffn_pre"][0]), np.asarray(inputs["norm_ffn_post"][0])], axis=0)),
        w_fg=f(inputs["w_ffn_gate"][0]),
        w_fu=f(inputs["w_ffn_up"][0]),
        w_fd=f(inputs["w_ffn_down"][0]),
    )
    for k_, v_ in consts.items():
        shared[k_] = f(v_)
    xs = np.asarray(inputs["x"], dtype=np.float32)
    maps = []
    for b in range(n_cores):
        m = dict(shared)
        m["x"] = np.ascontiguousarray(xs[b])
        maps.append(m)
    return maps


def kernel(**inputs):
    nc = _get_nc()
    in_maps = make_in_maps(inputs, 8)
    res = run_bass_kernel_spmd(nc, in_maps, core_ids=list(range(8)))
    out = np.stack([np.asarray(r["y"], dtype=np.float32) for r in res.results], axis=0)
    return out
```

```python
import numpy as np
from contextlib import ExitStack
import concourse.bass as bass
import concourse.mybir as mybir
from concourse.bass_utils import run_bass_kernel_spmd

F32 = mybir.dt.float32
BF16 = mybir.dt.bfloat16
ALU = mybir.AluOpType
AF = mybir.ActivationFunctionType

S = 2048
D = 1024
NT = 16
DFF = 2816
NF = 22
EPS = 1e-6
POOL_R = (1, 2, 4, 8)
GROUP_DIL = (1, 4, 16)

ARENA_WORDS = 52900
KNOB = dict(pairs=4, groups=3, attn=True, norm=True, njobs=32, sub=15, rope=9, v=0)


_CACHE = {}


class Tracker:
    def __init__(self, nc, es):
        self.nc = nc
        self.es = es
        self.q = {e: [] for e in ("pe", "act", "dve", "pool", "sp")}
        self.sem = {}
        self.cnt = {}
        self.seen = {e: {} for e in self.q}
        self.lastw = {}
        self.readers = {}
        self.log = []
        for e in ("pe", "act", "dve", "pool"):
            self._mk(e)

    def _mk(self, name):
        self.sem[name] = self.es.enter_context(self.nc.semaphore(name))
        self.cnt[name] = 0

    def _deps(self, reads, writes):
        deps = []
        for r in reads:
            t = self.lastw.get(r)
            if t:
                deps.append(t)
        for w in writes:
            t = self.lastw.get(w)
            if t:
                deps.append(t)
            deps.extend(self.readers.get(w, {}).items())
        return deps

    def _waits(self, eng, deps):
        need = {}
        for (s, v) in deps:
            if eng == "pe" and s == "pe":
                continue
            if s == "pe" and v > self.cnt["pe"]:
                raise AssertionError("wait on a future PE mark")
            if self.seen[eng].get(s, 0) < v:
                need[s] = max(need.get(s, 0), v)
        for s, v in need.items():
            self.seen[eng][s] = v
            sem = self.sem[s]
            self.log.append((eng, "wait", s, v))
            self.q[eng].append(lambda e, sem=sem, v=v: e.wait_ge(sem, v))

    def _record(self, tok, reads, writes):
        for r in reads:
            d = self.readers.setdefault(r, {})
            d[tok[0]] = max(d.get(tok[0], 0), tok[1])
        for w in writes:
            self.lastw[w] = tok
            self.readers[w] = {}

    def op(self, eng, fn, reads=(), writes=(), mark=True):
        psr = tuple(r for r in reads if isinstance(r, tuple) and r[0] == "ps")
        if psr:
            writes = tuple(writes) + psr
        self._waits(eng, self._deps(reads, writes))
        if mark:
            self.cnt[eng] += 1
            sem = self.sem[eng]
            self.q[eng].append(lambda e, fn=fn, sem=sem: fn(e).then_inc(sem, 1))
            tok = (eng, self.cnt[eng])
        else:
            self.q[eng].append(lambda e, fn=fn: fn(e))
            tok = (eng, self.cnt[eng] + 1)
        self.log.append((eng, "op", tok, mark))
        self._record(tok, reads, writes)

    def dma(self, qeng, key, fn, reads=(), writes=()):
        if key not in self.sem:
            self._mk(key)
        self._waits(qeng, self._deps(reads, writes))
        self.cnt[key] += 16
        sem = self.sem[key]
        self.q[qeng].append(lambda e, fn=fn, sem=sem: fn(e).then_inc(sem, 16))
        self.log.append((qeng, "dma", (key, self.cnt[key]), True))
        self._record((key, self.cnt[key]), reads, writes)

    def retoken(self, key, resources):
        for r in resources:
            self.lastw[r] = (key, self.cnt[key])

    def barrier(self):
        deps = [(s, c) for s, c in self.cnt.items() if c > 0]
        for eng in self.q:
            self._waits(eng, deps)


def build_program(dbg=None, stop_after=99):
    nc = bass.Bass("TRN2", target_bir_lowering=False)

    def din(name, shape):
        return nc.dram_tensor(name, list(shape), F32, kind="ExternalInput").ap()

    x = din("x", (S, D))
    w_in = din("w_in", (D, 7168))
    w_pgrp = din("w_pgrp", (512, 128))
    pscale = din("pscale", (128, 4))
    w_pbr = din("w_pbr", (512, D))
    w_abr = din("w_abr", (512, D))
    w_out = din("w_out", (D, D))
    gains = din("gains", (4, D))
    w_fg = din("w_fg", (D, DFF))
    w_fu = din("w_fu", (D, DFF))
    w_fd = din("w_fd", (DFF, D))
    c_cs = din("c_cs", (128, 4096))
    c_bf = din("c_bf", (128, 704))
    c_band = din("c_band", (128, 4608))
    c_icnt = din("c_icnt", (128, 1536))
    c_mask = din("c_mask", (128, 1024))
    y = nc.dram_tensor("y", [S, D], F32, kind="ExternalOutput").ap()
    dbg_out = {}
    if dbg:
        for name, shape in dbg.items():
            dbg_out[name] = nc.dram_tensor("dbg_" + name, list(shape), F32, kind="ExternalOutput").ap()

    with ExitStack() as es:
        arena = es.enter_context(nc.sbuf_tensor("arena", [128, ARENA_WORDS], F32))
        ps = es.enter_context(nc.psum_tensor("ps", [128, 4096], F32))
        tk = Tracker(nc, es)
        _CACHE['tk'] = tk

        def f32v(off, n):
            return arena[:, off:off + n]

        def bfv(off, nwords):
            return arena[:, off:off + nwords].bitcast(BF16)

        def bank(k, n=1):
            return ps[:, k * 512:(k + n) * 512]

        def MM(out, lhsT, rhs, start=True, stop=True, reads=(), writes=(), mark=False):
            tk.op("pe", lambda e: e.matmul(out, lhsT=lhsT, rhs=rhs, start=start, stop=stop), reads, writes, mark)

        def TR(out, in_, reads=(), writes=(), mark=False):
            tk.op("pe", lambda e: e.transpose(out, in_, ident), reads, writes, mark)

        def ACT(out, in_, func, reads=(), writes=(), scale=None, accum_out=None):
            kw = {}
            if scale is not None:
                kw["scale"] = scale
            if accum_out is not None:
                kw["accum_out"] = accum_out
            tk.op("act", lambda e: e.activation(out=out, in_=in_, func=func, **kw), reads, writes)

        def TT(out, in0, in1, op, reads=(), writes=(), eng="dve"):
            tk.op(eng, lambda e: e.tensor_tensor(out=out, in0=in0, in1=in1, op=op), reads, writes)

        def STT(out, in0, scalar, in1, op0, op1, reads=(), writes=()):
            tk.op("dve", lambda e: e.scalar_tensor_tensor(out=out, in0=in0, scalar=scalar, in1=in1, op0=op0, op1=op1),
                  reads, writes)

        def TS(out, in0, s1, s2, op0, op1=None, reads=(), writes=()):
            if op1 is None:
                tk.op("dve", lambda e: e.tensor_scalar(out=out, in0=in0, scalar1=s1, scalar2=None, op0=op0), reads, writes)
            else:
                tk.op("dve", lambda e: e.tensor_scalar(out=out, in0=in0, scalar1=s1, scalar2=s2, op0=op0, op1=op1),
                      reads, writes)

        def VCOPY(out, in_, reads=(), writes=()):
            tk.op("dve", lambda e: e.tensor_copy(out=out, in_=in_), reads, writes)

        def RCP(out, in_, reads=(), writes=()):
            tk.op("dve", lambda e: e.reciprocal(out=out, in_=in_), reads, writes)

        def MEMSET(out, val, reads=(), writes=()):
            tk.op("dve", lambda e: e.memset(out, val), reads, writes)

        def DMA(qeng, key, out, in_, reads=(), writes=()):
            tk.dma(qeng, key, lambda e: e.dma_start(out=out, in_=in_), reads, writes)

        dump_n = [0]

        def DUMP(name, src, res):
            if name in dbg_out:
                tk.barrier()
                dst = dbg_out[name]
                DMA("pool", "c0", dst, src, reads=res, writes=("dbg" + name,))
                tk.barrier()

        O_HT = 0
        O_C1 = 8192
        O_SM = 13312
        A = 13824
        hT = bfv(O_HT, 8192).rearrange("p (c t) -> p c t", c=8)
        smbf = bfv(O_SM, 352)
        rotR = smbf[:, 0:128]
        ident = smbf[:, 128:256]
        mask3 = smbf[:, 256:640]
        onesb = smbf[:, 640:704]
        psc = f32v(O_SM + 352, 4)
        stat = f32v(O_SM + 360, 128)
        mixedT = bfv(A + 0, 4096).rearrange("p (g t) -> p g t", g=4)
        oattnT = bfv(A + 4096, 4096).rearrange("p (j t) -> p j t", j=4)
        ostg = bfv(A + 8192, 1024)

        stat_i = [0]

        epsb = stat[:, 120:121]
        MEMSET(epsb, EPS, writes=("epsb",))
        onesf = stat[:, 56:120]
        MEMSET(onesf, 1.0, writes=("onesf",))
        ACT(stat[:, 121:122], epsb, AF.Ln, reads=("epsb",), writes=("warm",))
        ACT(stat[:, 122:123], epsb, AF.Exp, reads=("epsb",), writes=("warm",))

        def rstd_act(src_ap, src_res, junk_ap, junk_res):
            k = stat_i[0] % 14
            stat_i[0] += 1
            st = stat[:, k * 4:(k + 1) * 4]
            r = ("stat", k)
            tk.op("pool", lambda e: e.memset(st[:, 0:1], 0.0), (), (r,))
            ACT(junk_ap, src_ap, AF.Square, reads=src_res + (r,), writes=junk_res + (r,), accum_out=st[:, 0:1])
            tk.op("act", lambda e: e.activation(out=st[:, 1:2], in_=st[:, 0:1], func=AF.Ln, bias=epsb, scale=1.0 / D),
                  (r, "epsb"), (r,))
            ACT(st[:, 2:3], st[:, 1:2], AF.Exp, reads=(r,), writes=(r,), scale=-0.5)
            return st[:, 2:3], r

        g0 = f32v(O_C1, 1024)
        band = bfv(O_C1 + 1024, 2304).rearrange("p (g v e t) -> p g v e t", g=4, v=3, e=3)
        icnt = f32v(O_C1 + 3328, 1536).rearrange("p (g v t) -> p g v t", g=4, v=3)
        DMA("pool", "c0", smbf, c_bf, writes=("smbf",))
        DMA("pool", "c0", band.rearrange("p g v e t -> p (g v e t)"), c_band, writes=("C1b",))
        tk.retoken("c0", ("smbf", "C1b"))

        def late_consts():
            DMA("sp", "c2", g0, gains[0:1, :].partition_broadcast(128), writes=("C1g",))

        def late_consts2():
            DMA("sp", "c1", psc, pscale, writes=("psc",))
            DMA("sp", "c1", f32v(O_C1 + 3328, 1536), c_icnt, writes=("C1i",))
            tk.retoken("c1", ("psc", "C1i"))

        Wu = bfv(A + 12288, 2048).rearrange("p (c f) -> p c f", c=8)
        wgrp = bfv(A + 14336, 256).rearrange("p (g d) -> p g d", g=4)
        DMA("pool", "w6", Wu, w_in[:, 0:512].rearrange("(c p) f -> p c f", p=128), writes=("Wu",))
        DMA("pool", "w7", wgrp, w_pgrp.rearrange("(g c) d -> c g d", c=128), writes=("wgrp",))
        wring = [bfv(A + 34096 + i * 512, 512).rearrange("p (c f) -> p c f", c=8) for i in range(6)]
        units = [(jp, g) for jp in range(4) for g in range(3)]
        wsl = {}

        def issue_weights(u):
            jp, g = units[u]
            fc = g * 4 + jp
            ws = []
            for m, off in enumerate((512, 2048, 3584)):
                s_ = (u % 2) * 3 + m
                DMA("pool", "w%d" % s_, wring[s_],
                    w_in[:, off + fc * 128: off + (fc + 1) * 128].rearrange("(c p) f -> p c f", p=128),
                    writes=(("wr", s_),))
                ws.append(s_)
            wsl[u] = ws


        xring = [f32v(A + 24576 + i * 1024, 1024) for i in range(3)]
        hbf = [bfv(A + 27648 + i * 512, 512) for i in range(3)]
        gb = [0]

        def gbank():
            k = gb[0] % 8
            gb[0] += 1
            return k

        def to_featmajor(i, src_bf, src_res, dstT):
            k = gbank()
            pb = bank(k).bitcast(BF16).rearrange("p (c t) -> p c t", c=8)
            for c in range(8):
                TR(pb[:, c, :], src_bf[:, c * 128:(c + 1) * 128], reads=src_res + ("smbf",), writes=(("ps", k),),
                   mark=(c == 7))
            ACT(dstT[:, :, i * 128:(i + 1) * 128], pb, AF.Copy, reads=(("ps", k),), writes=(("hT", i),))

        u_tok = bfv(A + 16384, 4096).rearrange("p (i f) -> p i f", i=16)
        p1st = {}

        def p1_s1(i):
            sl = i % 3
            DMA("sp", "x%d" % sl, xring[sl], x[i * 128:(i + 1) * 128, :], writes=(("xr", sl),))
            hb = hbf[i % 3]
            p1st[i] = rstd_act(xring[sl], (("xr", sl),), hb, (("hbf", i % 3),))

        def p1_s2(i):
            sl = i % 3
            hb = hbf[i % 3]
            rs, rr = p1st.pop(i)
            STT(hb, xring[sl], rs, g0, ALU.mult, ALU.mult, reads=(("xr", sl), rr, "C1g"), writes=(("hbf", i % 3),))

        def p1_s3(i):
            hb = hbf[i % 3]
            k = i % 4
            pb = bank(k).bitcast(BF16).rearrange("p (c t) -> p c t", c=8)
            for c in range(8):
                TR(pb[:, c, :], hb[:, c * 128:(c + 1) * 128], reads=(("hbf", i % 3), "smbf"), writes=(("ps", k),),
                   mark=(c == 7))

        def p1_s4(i):
            k = i % 4
            pb = bank(k).bitcast(BF16).rearrange("p (c t) -> p c t", c=8)
            if i % 2 == 0:
                ACT(hT[:, :, i * 128:(i + 1) * 128], pb, AF.Copy, reads=(("ps", k),), writes=(("hT", i),))
            else:
                VCOPY(hT[:, :, i * 128:(i + 1) * 128], pb, reads=(("ps", k),), writes=(("hT", i),))

        def p1_s5(i):
            k = 4 + i % 4
            for c in range(8):
                MM(bank(k), hT[:, c, i * 128:(i + 1) * 128], Wu[:, c, :], start=(c == 0), stop=(c == 7),
                   reads=(("hT", i), "Wu"), writes=(("ps", k),), mark=(c == 7))

        def p1_s6(i):
            k = 4 + i % 4
            VCOPY(u_tok[:, i, :], bank(k), reads=(("ps", k),), writes=(("u", i),))

        for t in range(NT + 5):
            for stg, fn in enumerate((p1_s1, p1_s2, p1_s3, p1_s4, p1_s5, p1_s6)):
                if 0 <= t - stg < NT:
                    fn(t - stg)
                if t == 0 and stg == 0:
                    late_consts()
                if t == 2 and stg == 0:
                    late_consts2()
        HT_ALL = tuple(("hT", i) for i in range(NT))
        DUMP("hT", hT[:, 0, :], HT_ALL)

        def _rest():
            grpT = bfv(A + 20480, 4096).rearrange("p (g t) -> p g t", g=4)
            issue_weights(0)
            issue_weights(1)
            for g in range(4):
                for tb in range(4):
                    k = gbank()
                    for j in range(4):
                        i = 4 * tb + j
                        var = 0 if i == 0 else (2 if i == NT - 1 else 1)
                        es_ = [e for e in (-1, 0, 1) if 0 <= i + e < NT]
                        for n_, e in enumerate(es_):
                            MM(bank(k)[:, j * 128:(j + 1) * 128], u_tok[:, i + e, g * 128:(g + 1) * 128],
                               band[:, g, var, e + 1, :], start=(n_ == 0), stop=(n_ == len(es_) - 1),
                               reads=(("u", i + e), "C1b"), writes=(("ps", k),), mark=(j == 3 and n_ == len(es_) - 1))
                    inv = 1.0 / (2 * POOL_R[g] + 1)
                    runs = {0: [(1, 4)], 3: [(0, 3)]}.get(tb, [(0, 4)])
                    for (j0, j1) in runs:
                        i0_ = 4 * tb + j0
                        i1_ = 4 * tb + j1
                        tk.op("act", lambda e, o_=grpT[:, g, i0_ * 128:i1_ * 128], i_=bank(k)[:, j0 * 128:j1 * 128],
                              m_=inv: e.mul(out=o_, in_=i_, mul=m_),
                              (("ps", k),), tuple(("grp", g, ii) for ii in range(i0_, i1_)))
                    for j in range(4):
                        i = 4 * tb + j
                        if i not in (0, NT - 1):
                            continue
                        var = 0 if i == 0 else 2
                        TT(grpT[:, g, i * 128:(i + 1) * 128], bank(k)[:, j * 128:(j + 1) * 128], icnt[:, g, var, :], ALU.mult,
                           reads=(("ps", k), "C1i"), writes=(("grp", g, i),))
            for g in range(4):
                for tb in range(4):
                    k = gbank()
                    MM(bank(k), wgrp[:, g, :], grpT[:, g, tb * 512:(tb + 1) * 512],
                       reads=tuple(("grp", g, 4 * tb + j) for j in range(4)) + ("wgrp",), writes=(("ps", k),), mark=True)
                    TS(mixedT[:, g, tb * 512:(tb + 1) * 512], bank(k), psc[:, g:g + 1], None, ALU.mult,
                       reads=(("ps", k), "psc"), writes=(("mixT", g, tb),))
            DUMP("mixedT", mixedT[:, 0, :], tuple(("mixT", 0, tb) for tb in range(4)))

            yield
            tk.barrier()
            cosT = f32v(O_C1, 2048)
            sinT = f32v(O_C1 + 2048, 2048)
            DMA("sp", "c1", f32v(O_C1, 4096), c_cs, writes=("C1cs",))
            QT = [bfv(A + 12288 + g * 1024, 1024) for g in range(3)]
            KT = [bfv(A + 15360 + g * 1024, 1024) for g in range(3)]
            VTc = [bfv(A + 18432 + i * 1024, 1024) for i in range(2)]
            Vp = [bfv(A + 20480 + g * 1040, 1040).rearrange("p (n h d) -> p n h d", n=16, h=2) for g in range(3)]
            Osum = [f32v(A + 23600 + i * 2048, 2048) for i in range(2)]
            maskG = bfv(A + 9216, 512)
            PT = [bfv(A + 9728 + i * 256, 256) for i in range(4)]
            DMA("pool", "c0", maskG, c_mask, writes=("maskG",))
            abf = [bfv(A + 28464 + i * 256, 256) for i in range(2)]
            t12 = [f32v(A + 28976 + i * 512, 512) for i in range(6)]
            Wab = bfv(A + 32048, 2048).rearrange("p (j d) -> p j d", j=4)
            for g in range(3):
                MEMSET(Vp[g][:, :, :, 64:65], 1.0, writes=(("Vp", g),))

            def cm_view(buf, g, tb):
                d = GROUP_DIL[g]
                if d == 1:
                    return buf[:, tb * 512:(tb + 1) * 512]
                w = 512 // d
                return buf.rearrange("p (r l) -> p r l", r=d)[:, :, tb * w:(tb + 1) * w]

            def nat_src(ap, g):
                d = GROUP_DIL[g]
                if d == 1:
                    return ap
                return ap.rearrange("p (a r) -> p r a", r=d)

            st_i = [0]
            ot_i = [0]
            gen3 = [0]
            pt_i = [0]
            ab_i = [0]
            t_i = [0]
            vt_i = [0]

            def g3bank():
                k = 5 + gen3[0] % 2
                gen3[0] += 1
                return k

            G2 = 7

            def proj_steps(u):
                jp, g = units[u]
                ws = wsl[u]
                pending = [None]
                vt = vt_i[0] % 2
                vt_i[0] += 1

                def proj_mm(k, s_, tb):
                    for c in range(8):
                        MM(bank(k), wring[s_][:, c, :], hT[:, c, tb * 512:(tb + 1) * 512], start=(c == 0),
                           stop=(c == 7), reads=(("wr", s_),) + HT_ALL[4 * tb:4 * tb + 4], writes=(("ps", k),),
                           mark=(c == 7))

                def rope_finish():
                    if pending[0] is None:
                        return
                    ta, a_, dstbuf, nm, tb = pending[0]
                    pending[0] = None
                    k2 = G2
                    MM(bank(k2), rotR, abf[a_], reads=(("abf", a_), "smbf"), writes=(("ps", k2),), mark=True)
                    tb2 = t_i[0] % 6
                    t_i[0] += 1
                    TT(t12[tb2], bank(k2), sinT[:, tb * 512:(tb + 1) * 512], ALU.mult,
                       reads=(("ps", k2), "C1cs"), writes=(("t12", tb2),))
                    TT(cm_view(dstbuf, g, tb), nat_src(t12[ta], g), nat_src(t12[tb2], g), ALU.add,
                       reads=(("t12", ta), ("t12", tb2)), writes=((nm, g),), eng="dve")

                def qk_block(m, dstbuf, nm, tb):
                    st = {}

                    def pe():
                        st["k"] = g3bank()
                        proj_mm(st["k"], ws[m], tb)

                    def post():
                        k = st["k"]
                        a_ = ab_i[0] % 2
                        ab_i[0] += 1
                        ACT(abf[a_], bank(k), AF.Copy, reads=(("ps", k),), writes=(("abf", a_),))
                        ta = t_i[0] % 6
                        t_i[0] += 1
                        TT(t12[ta], bank(k), cosT[:, tb * 512:(tb + 1) * 512], ALU.mult,
                           reads=(("ps", k), "C1cs"), writes=(("t12", ta),))
                        rope_finish()
                        pending[0] = (ta, a_, dstbuf, nm, tb)
                    return pe, post

                def v_block(tb):
                    st = {}

                    def pe():
                        st["k"] = g3bank()
                        proj_mm(st["k"], ws[2], tb)

                    def post():
                        k = st["k"]
                        rope_finish()
                        ACT(cm_view(VTc[vt], g, tb), nat_src(bank(k), g), AF.Copy, reads=(("ps", k),),
                            writes=(("VTc", vt),))
                    return pe, post

                def vtr(half):
                    st = {}

                    def pe():
                        k = st["k"] = g3bank()
                        pb = bank(k).bitcast(BF16).rearrange("p (n f) -> p n f", n=8)
                        for j in range(8):
                            n = half * 8 + j
                            TR(pb[:, j, :], VTc[vt][:, n * 128:(n + 1) * 128], reads=(("VTc", vt), "smbf"),
                               writes=(("ps", k),), mark=(j == 7))

                    def post():
                        k = st["k"]
                        pb = bank(k).bitcast(BF16).rearrange("p (n f) -> p n f", n=8)
                        ACT(Vp[g][:, half * 8:(half + 1) * 8, :, 0:64], pb.rearrange("p n (h d) -> p n h d", h=2),
                            AF.Copy, reads=(("ps", k),), writes=(("Vp", g),))
                    return pe, post

                steps = []
                for tb in range(4):
                    steps.append(qk_block(0, QT[g], "QT", tb))
                for tb in range(4):
                    steps.append(qk_block(1, KT[g], "KT", tb))
                for tb in range(4):
                    steps.append(v_block(tb))
                steps.append(vtr(0))
                steps.append(vtr(1))
                return steps

            def att_steps(u):
                jp, g = units[u]
                d = GROUP_DIL[g]
                T = NT // d
                jobs = [(hh, sb, qt) for hh in range(2) for sb in range(4) for qt in range(4)]
                if T == 1:
                    GT, W, moff = 2, 128, 512
                    COLS = {0: (0, 128)}
                else:
                    GT, W, moff = 2, 256, 0
                    COLS = {-1: (0, 64), 0: (64, 192), 1: (192, 256)}
                QSUB = {-1: (0, 64), 0: (0, 128), 1: (64, 128)}
                groups = [list(range(a, a + GT)) for a in range(0, len(jobs), GT)]
                stres = {}
                otk = [None]

                def tile_es(ji):
                    hh, sb, qt = jobs[ji]
                    n = 4 * sb + qt
                    qc = n % T
                    return n, [e for e in (-1, 0, 1) if 0 <= qc + e < T]

                def emit_st(gi):
                    k = st_i[0] % 3
                    st_i[0] += 1
                    MM(bank(k)[:, 0:GT * W], ident, maskG[:, moff:moff + GT * W], start=True, stop=False,
                       reads=("smbf", "maskG"), writes=(("ps", k),))
                    todo = []
                    for m, ji in enumerate(groups[gi]):
                        hh, sb, qt = jobs[ji]
                        n, es_ = tile_es(ji)
                        pr = slice(hh * 64, (hh + 1) * 64)
                        for e in es_:
                            todo.append((m, n, e, pr))
                    for n_, (m, n, e, pr) in enumerate(todo):
                        a0, a1 = COLS[e]
                        q0, q1 = QSUB[e]
                        MM(bank(k)[:, m * W + a0:m * W + a1], KT[g][pr, (n + e) * 128:(n + e + 1) * 128],
                           QT[g][pr, n * 128 + q0:n * 128 + q1], start=False, stop=(n_ == len(todo) - 1),
                           reads=(("QT", g), ("KT", g)), writes=(("ps", k),), mark=(n_ == len(todo) - 1))
                    stres[gi] = k

                sts = {}

                def pv(gi):
                    p_ = sts.pop(gi)
                    for m, ji in enumerate(groups[gi]):
                        hh, sb, qt = jobs[ji]
                        n, es_ = tile_es(ji)
                        if qt == 0:
                            otk[0] = 3 + ot_i[0] % 2
                            ot_i[0] += 1
                        ok = otk[0]
                        order = [0] + [e for e in es_ if e != 0]
                        for n_, e in enumerate(order):
                            a0, a1 = COLS[e]
                            q0, q1 = QSUB[e]
                            MM(bank(ok)[0:65, qt * 128 + q0:qt * 128 + q1], Vp[g][:, n + e, hh, :],
                               PT[p_][:, m * W + a0:m * W + a1], start=(n_ == 0), stop=(n_ == len(order) - 1),
                               reads=(("PT", p_), ("Vp", g)), writes=(("ps", ok),), mark=(n_ == len(order) - 1))
                        if qt == 3:
                            osl = Osum[hh]
                            src = bank(ok)[0:65, :]
                            if d == 1:
                                dst = osl[0:65, sb * 512:(sb + 1) * 512]
                            elif d == 4:
                                dst = osl[0:65, :].rearrange("p (l r) -> p r l", r=4)[:, sb, :]
                            else:
                                dst = osl[0:65, :].rearrange("p (l r) -> p r l", r=16)[:, 4 * sb:4 * sb + 4, :]
                                src = src.rearrange("p (r l) -> p r l", r=4)
                            if g == 0:
                                ACT(dst, src, AF.Copy, reads=(("ps", ok),), writes=(("Osum", hh),))
                            else:
                                TT(dst, src, dst, ALU.add, reads=(("ps", ok), ("Osum", hh)), writes=(("Osum", hh),))

                NG = len(groups)

                def mk(gi):
                    def head():
                        if gi == 0:
                            for q_ in range(min(3, NG)):
                                emit_st(q_)
                        k = stres[gi]
                        p_ = pt_i[0] % 4
                        pt_i[0] += 1
                        sts[gi] = p_
                        ACT(PT[p_][:, 0:GT * W], bank(k)[:, 0:GT * W], AF.Exp, reads=(("ps", k),), writes=(("PT", p_),),
                            scale=0.125)

                    def tail():
                        if gi >= 1:
                            pv(gi - 1)
                        if gi + 3 < NG:
                            emit_st(gi + 3)
                        if gi == NG - 1:
                            pv(gi)
                    return head, tail

                return [mk(gi) for gi in range(NG)]

            rtmp = [f32v(A + 10752 + i * 512, 512) for i in range(2)]
            rt_i = [0]

            def norm_steps(jp, hh, banks=(7,)):
                osl = Osum[hh]
                ro = ("Osum", hh)

                def blk(tb):
                    def f():
                        k = banks[(hh * 4 + tb) % len(banks)]
                        MM(bank(k)[0:64, :], onesf[64:65, :], osl[64:65, tb * 512:(tb + 1) * 512], start=True, stop=True,
                           reads=(ro, "onesf"), writes=(("ps", k),), mark=True)
                        r_ = rt_i[0] % 2
                        rt_i[0] += 1
                        ACT(rtmp[r_][0:64, :], bank(k)[0:64, :], AF.Ln, reads=(("ps", k),), writes=(("rtmp", r_),))
                        ACT(rtmp[r_][0:64, :], rtmp[r_][0:64, :], AF.Exp, reads=(("rtmp", r_),), writes=(("rtmp", r_),),
                            scale=-1.0)
                        if hh == 0:
                            TT(oattnT[0:64, jp, tb * 512:(tb + 1) * 512], rtmp[r_][0:64, :],
                               osl[0:64, tb * 512:(tb + 1) * 512], ALU.mult, reads=(("rtmp", r_), ro),
                               writes=(("oat", jp, 0),))
                        else:
                            TT(ostg[0:64, tb * 512:(tb + 1) * 512], rtmp[r_][0:64, :],
                               osl[0:64, tb * 512:(tb + 1) * 512], ALU.mult, reads=(("rtmp", r_), ro), writes=("ostg",))
                            if tb == 3:
                                DMA("sp", "os", oattnT[64:128, jp, :], ostg[0:64, :], reads=("ostg",),
                                    writes=(("oat", jp, 1),))
                    return f
                return [blk(tb) for tb in range(4)]

            NU = len(units)
            gpre = [bfv(O_C1 + 4096 + i * 512, 512).rearrange("p (c f) -> p c f", c=8) for i in range(2)]
            for pe_, post_ in proj_steps(0):
                pe_()
                post_()
            for u in range(NU):
                jp, g = units[u]
                Asteps = att_steps(u)
                Psteps = proj_steps(u + 1) if u + 1 < NU else []
                if u + 2 < NU:
                    issue_weights(u + 2)
                if u == NU - 3:
                    DMA("pool", "w7", Wab, w_abr.rearrange("(j p) d -> p j d", p=128), writes=("Wab",))
                if u == NU - 2:
                    for i_, off in enumerate((5120, 6144)):
                        DMA("pool", "gk%d" % i_, gpre[i_],
                            w_in[:, off:off + 128].rearrange("(c p) f -> p c f", p=128), writes=(("gr", 6 + i_),))
                nA, nP = len(Asteps), len(Psteps)
                pos = {}
                GRP = KNOB.get("pgrp", 1)
                ngrp = (nP + GRP - 1) // GRP
                for j in range(nP):
                    pos.setdefault(int((j // GRP + 0.5) * nA / max(ngrp, 1)), []).append(j)
                postq = []
                extra = {}
                if g == 0 and jp >= 1:
                    n0 = norm_steps(jp - 1, 0)
                    n1 = norm_steps(jp - 1, 1)
                    for ii, f_ in zip((0, 1, 1, 2), n0):
                        extra.setdefault(ii, []).append(f_)
                    for ii, f_ in zip((3, 4, 5, 6), n1):
                        extra.setdefault(ii, []).append(f_)
                for i, (head, tail) in enumerate(Asteps):
                    head()
                    for f_ in extra.get(i, ()):
                        f_()
                    for _, p_ in postq:
                        p_()
                    postq = []
                    for j in pos.get(i, ()):
                        keep = []
                        for jj, p_ in postq:
                            if jj <= j - 2:
                                p_()
                            else:
                                keep.append((jj, p_))
                        postq = keep
                        Psteps[j][0]()
                        postq.append((j, Psteps[j][1]))
                    tail()
                for _, p_ in postq:
                    p_()
                if u == NU - 1:
                    for hh_ in range(2):
                        for s_ in norm_steps(jp, hh_, banks=(5, 6, 7)):
                            s_()
            DUMP("QT0", QT[0], (("QT", 0),))
            DUMP("oattnT", oattnT[0:64, 0, :], (("oat", 0, 0),))

            yield
            tk.barrier()
            msumT = bfv(A + 12288, 8192).rearrange("p (c t) -> p c t", c=8)
            Wpb = bfv(A + 20480, 2048).rearrange("p (g d) -> p g d", g=4)
            gring = [bfv(A + 26624 + i * 512, 512).rearrange("p (c f) -> p c f", c=8) for i in range(6)] + gpre
            sg = [f32v(A + 29696 + i * 512, 512) for i in range(4)]
            tt_ = [f32v(A + 22528 + i * 512, 512) for i in range(4)]
            for j in range(3):
                DMA("sp", "c1", f32v(O_C1 + j * 1024, 1024), gains[j + 1:j + 2, :].partition_broadcast(128), writes=(("C1g", j),))
            tk.retoken("c1", tuple(("C1g", j) for j in range(3)))
            Wout = bfv(A + 34096, 4096).rearrange("p (c d) -> p c d", c=8)
            gs_i = [0]

            def issue_gates(dc):
                gsl = []
                for off in (5120, 6144):
                    s_ = gs_i[0] % 6
                    gs_i[0] += 1
                    DMA("pool", "w%d" % s_, gring[s_],
                        w_in[:, off + dc * 128: off + (dc + 1) * 128].rearrange("(c p) f -> p c f", p=128),
                        writes=(("gr", s_),))
                    gsl.append(s_)
                return gsl

            gsl_next = [6, 7]
            DMA("pool", "w6", Wpb, w_pbr.rearrange("(g c) d -> c g d", c=128), writes=("Wpb",))
            sg_i = [0]
            for dc in range(8):
                if dc == 1:
                    DMA("pool", "w6", Wout, w_out.rearrange("(c p) d -> p c d", p=128), reads=("Wpb",), writes=("Wout",))
                gsl = gsl_next
                if dc + 1 < 8:
                    gsl_next = issue_gates(dc + 1)
                for tb in range(4):
                    tsl = slice(tb * 512, (tb + 1) * 512)
                    kA, kB, kC, kD = gbank(), gbank(), gbank(), gbank()
                    for kk, s_ in ((kC, gsl[0]), (kD, gsl[1])):
                        for c in range(8):
                            MM(bank(kk), gring[s_][:, c, :], hT[:, c, tsl], start=(c == 0), stop=(c == 7),
                               reads=(("gr", s_),) + HT_ALL[4 * tb:4 * tb + 4], writes=(("ps", kk),), mark=(c == 7))
                    for j in range(4):
                        MM(bank(kB), Wab[:, j, dc * 128:(dc + 1) * 128], oattnT[:, j, tsl], start=(j == 0),
                           stop=(j == 3), reads=("Wab", ("oat", j, 0), ("oat", j, 1)), writes=(("ps", kB),),
                           mark=(j == 3))
                    for g in range(4):
                        MM(bank(kA), Wpb[:, g, dc * 128:(dc + 1) * 128], mixedT[:, g, tsl], start=(g == 0), stop=(g == 3),
                           reads=("Wpb", ("mixT", g, tb)), writes=(("ps", kA),), mark=(g == 3))
                    s0 = sg_i[0] % 4
                    s1 = (sg_i[0] + 1) % 4
                    sg_i[0] += 2
                    ACT(sg[s0], bank(kC), AF.Sigmoid, reads=(("ps", kC),), writes=(("sg", s0),))
                    ACT(sg[s1], bank(kD), AF.Sigmoid, reads=(("ps", kD),), writes=(("sg", s1),))
                    TT(tt_[s0], bank(kA), sg[s0], ALU.mult, reads=(("ps", kA), ("sg", s0)), writes=(("tt", s0),))
                    TT(tt_[s1], bank(kB), sg[s1], ALU.mult, reads=(("ps", kB), ("sg", s1)), writes=(("tt", s1),))
                    TT(msumT[:, dc, tsl], tt_[s0], tt_[s1], ALU.add, reads=(("tt", s0), ("tt", s1)), writes=(("ms", dc, tb),))
            DUMP("msumT", msumT[:, 0, :], tuple(("ms", 0, tb) for tb in range(4)))

            yield
            tk.barrier()
            xring2 = [f32v(A + 20480 + i * 1024, 1024) for i in range(3)]
            tileA = [f32v(A + 23552 + i * 1024, 1024) for i in range(2)]
            tileB = [f32v(A + 25600 + i * 1024, 1024) for i in range(4)]
            hbf2 = [bfv(A + 29696 + i * 512, 512) for i in range(3)]
            g1 = f32v(O_C1, 1024)
            g2 = f32v(O_C1 + 1024, 1024)
            g3 = f32v(O_C1 + 2048, 1024)
            h2T = hT
            big_i = [0]
            tl_i = [0]
            p4 = {}

            def p4_s1(i):
                sl = i % 3
                DMA("sp", "x%d" % sl, xring2[sl], x[i * 128:(i + 1) * 128, :], writes=(("xr", sl),))
                kb = (i % 3) * 2
                for half in range(2):
                    for c in range(8):
                        MM(bank(kb + half), msumT[:, c, i * 128:(i + 1) * 128], Wout[:, c, half * 512:(half + 1) * 512],
                           start=(c == 0), stop=(c == 7), reads=(("ms", c, i // 4), "Wout"),
                           writes=(("ps", kb + half),), mark=(c == 7))
                ta = i % 2
                pres = (("ps", kb), ("ps", kb + 1))
                rs, rr = rstd_act(bank(kb, 2), pres, tileA[ta], (("tA", ta),))
                p4[i] = dict(rs=rs, rr=rr, kb=kb, ta=ta, tb=i % 4, sl=sl)

            def p4_s2(i):
                d_ = p4[i]
                kb, ta, tb_, sl = d_["kb"], d_["ta"], d_["tb"], d_["sl"]
                pres = (("ps", kb), ("ps", kb + 1))
                STT(tileA[ta], bank(kb, 2), d_["rs"], g1, ALU.mult, ALU.mult, reads=pres + (d_["rr"], ("C1g", 0)),
                    writes=(("tA", ta),))
                TT(tileB[tb_], tileA[ta], xring2[sl], ALU.add, reads=(("tA", ta), ("xr", sl)), writes=(("tB", tb_),))
                DMA("sp", "ys%d" % (i % 4), y[i * 128:(i + 1) * 128, :], tileB[tb_], reads=(("tB", tb_),),
                    writes=(("y", i),))

            def p4_s3(i):
                d_ = p4[i]
                tb_ = d_["tb"]
                hb = hbf2[i % 3]
                d_["rs2"], d_["rr2"] = rstd_act(tileB[tb_], (("tB", tb_),), hb, (("hbf", i % 3),))

            def p4_s4(i):
                d_ = p4[i]
                tb_ = d_["tb"]
                hb = hbf2[i % 3]
                STT(hb, tileB[tb_], d_["rs2"], g2, ALU.mult, ALU.mult, reads=(("tB", tb_), d_["rr2"], ("C1g", 1)),
                    writes=(("hbf", i % 3),))

            def p4_s5(i):
                hb = hbf2[i % 3]
                k = 6 + i % 2
                pb = bank(k).bitcast(BF16).rearrange("p (c t) -> p c t", c=8)
                for c in range(8):
                    TR(pb[:, c, :], hb[:, c * 128:(c + 1) * 128], reads=(("hbf", i % 3), "smbf"), writes=(("ps", k),),
                       mark=(c == 7))

            def p4_s6(i):
                p4.pop(i)
                k = 6 + i % 2
                pb = bank(k).bitcast(BF16).rearrange("p (c t) -> p c t", c=8)
                if i % 4 == 3:
                    VCOPY(h2T[:, :, i * 128:(i + 1) * 128], pb, reads=(("ps", k),), writes=(("hT", i),))
                else:
                    ACT(h2T[:, :, i * 128:(i + 1) * 128], pb, AF.Copy, reads=(("ps", k),), writes=(("hT", i),))

            for t in range(NT + 5):
                for stg, fn in enumerate((p4_s1, p4_s2, p4_s3, p4_s4, p4_s5, p4_s6)):
                    if 0 <= t - stg < NT:
                        fn(t - stg)
            fpre = [bfv(O_C1 + 3072 + i * 512, 512).rearrange("p (c f) -> p c f", c=8) for i in range(4)]
            for f in range(2):
                for i_, wsrc in enumerate((w_fg, w_fu)):
                    DMA("pool", "w%d" % (2 * f + i_), fpre[2 * f + i_],
                        wsrc[:, f * 128:(f + 1) * 128].rearrange("(c p) f -> p c f", p=128),
                        writes=(("fr", 6 + 2 * f + i_),))
            DUMP("h2T", h2T[:, 0, :], HT_ALL)

            yield
            tk.barrier()
            hidT = bfv(A + 0, 22528).rearrange("p (f t) -> p f t", f=NF)
            fring = [bfv(A + 22528 + i * 512, 512).rearrange("p (c f) -> p c f", c=8) for i in range(6)] + fpre
            sring = [f32v(A + 26624 + i * 512, 512) for i in range(2)]
            Wd = bfv(A + 27648, 11264).rearrange("p (f d) -> p f d", f=NF)
            fs_i = [0]
            sr_i = [0]
            for f in range(NF):
                if f == 2:
                    DMA("pool", "w7", Wd, w_fd.rearrange("(f p) d -> p f d", p=128), writes=("Wd",))
                fsl = []
                for i_, wsrc in enumerate((w_fg, w_fu)):
                    if f < 2:
                        fsl.append(6 + 2 * f + i_)
                        continue
                    s_ = fs_i[0] % 6
                    fs_i[0] += 1
                    DMA("pool", "w%d" % s_, fring[s_],
                        wsrc[:, f * 128:(f + 1) * 128].rearrange("(c p) f -> p c f", p=128), writes=(("fr", s_),))
                    fsl.append(s_)
                for tb in range(4):
                    tsl = slice(tb * 512, (tb + 1) * 512)
                    kA, kB = gbank(), gbank()
                    for kk, s_ in ((kA, fsl[0]), (kB, fsl[1])):
                        for c in range(8):
                            MM(bank(kk), fring[s_][:, c, :], h2T[:, c, tsl], start=(c == 0), stop=(c == 7),
                               reads=(("fr", s_),) + HT_ALL[4 * tb:4 * tb + 4], writes=(("ps", kk),), mark=(c == 7))
                    s0 = sr_i[0] % 2
                    sr_i[0] += 1
                    ACT(sring[s0], bank(kA), AF.Silu, reads=(("ps", kA),), writes=(("sr", s0),))
                    TT(hidT[:, f, tsl], bank(kB), sring[s0], ALU.mult, reads=(("ps", kB), ("sr", s0)), writes=(("hid", f, tb),))
            DUMP("hidT", hidT[:, 0, :], tuple(("hid", 0, tb) for tb in range(4)))

            yield
            tk.barrier()
            xring3 = [f32v(O_HT + i * 1024, 1024) for i in range(3)]
            tile3 = [f32v(O_HT + 3072 + i * 1024, 1024) for i in range(4)]
            p5 = {}

            def p5_s1(i):
                sl = i % 3
                DMA("sp", "x%d" % sl, xring3[sl], y[i * 128:(i + 1) * 128, :], reads=(("y", i),), writes=(("xr", sl),))
                kb = (i % 4) * 2
                for half in range(2):
                    for f in range(NF):
                        MM(bank(kb + half), hidT[:, f, i * 128:(i + 1) * 128], Wd[:, f, half * 512:(half + 1) * 512],
                           start=(f == 0), stop=(f == NF - 1), reads=(("hid", f, i // 4), "Wd"),
                           writes=(("ps", kb + half),), mark=(f == NF - 1))
                ta = (2 * i) % 4
                pres = (("ps", kb), ("ps", kb + 1))
                rs, rr = rstd_act(bank(kb, 2), pres, tile3[ta], (("tile", ta),))
                p5[i] = dict(rs=rs, rr=rr, kb=kb, ta=ta, tb=(2 * i + 1) % 4, sl=sl)

            def p5_s2(i):
                d_ = p5.pop(i)
                kb, ta, tb_, sl = d_["kb"], d_["ta"], d_["tb"], d_["sl"]
                pres = (("ps", kb), ("ps", kb + 1))
                STT(tile3[ta], bank(kb, 2), d_["rs"], g3, ALU.mult, ALU.mult, reads=pres + (d_["rr"], ("C1g", 2)),
                    writes=(("tile", ta),))
                TT(tile3[tb_], tile3[ta], xring3[sl], ALU.add, reads=(("tile", ta), ("xr", sl)), writes=(("tile", tb_),))
                DMA("sp", "ys%d" % (i % 2), y[i * 128:(i + 1) * 128, :], tile3[tb_], reads=(("tile", tb_),),
                    writes=(("y", i),))

            for t in range(NT + 1):
                if t < NT:
                    p5_s1(t)
                if t >= 1:
                    p5_s2(t - 1)

            yield

        _g = _rest()
        _n = 1
        while _n < stop_after:
            try:
                next(_g)
            except StopIteration:
                break
            _n += 1
        tk.barrier()

        block = es.enter_context(nc.Block())

        @block.tensor
        def _(e):
            for f_ in tk.q["pe"]:
                f_(e)

        @block.scalar
        def _(e):
            for f_ in tk.q["act"]:
                f_(e)

        @block.vector
        def _(e):
            for f_ in tk.q["dve"]:
                f_(e)

        @block.gpsimd
        def _(e):
            for f_ in tk.q["pool"]:
                f_(e)

        @block.sync
        def _(e):
            for f_ in tk.q["sp"]:
                f_(e)

    return nc


def _constants():
    f32 = np.float32
    pos = np.arange(S, dtype=f32)
    pw = (10000.0 ** (np.arange(0, 64, 2, dtype=np.float64) / 64.0)).astype(f32)
    inv_freq = (f32(1.0) / pw).astype(f32)
    ang = (pos[:, None] * inv_freq[None, :]).astype(f32)
    cos = np.cos(ang).astype(f32)
    sin = np.sin(ang).astype(f32)
    p = np.arange(128)
    j = (p % 64) % 32
    c_cs = np.concatenate([cos[:, j].T, sin[:, j].T], axis=1).astype(f32)
    ident = np.eye(128, dtype=f32)
    rot = np.zeros((128, 128), f32)
    for m in range(128):
        if m % 64 < 32:
            rot[m + 32, m] = -1.0
        else:
            rot[m - 32, m] = 1.0
    i_ = np.arange(128)[:, None]
    c_ = np.arange(128)[None, :]
    mask3 = np.stack([(np.abs(128 * e + i_ - c_) <= 64).astype(f32) for e in (-1, 0, 1)], axis=1)
    ones = np.ones((128, 64), f32)
    negmask = (mask3 - 1.0) * 30000.0
    negmask = np.concatenate([negmask[:, 0, 0:64], negmask[:, 1, :], negmask[:, 2, 64:128], np.zeros((128, 128), f32)], axis=1)
    c_bf = np.concatenate([rot, ident, negmask, ones], axis=1)
    band = np.zeros((128, 4, 3, 3, 128), f32)
    icnt = np.zeros((128, 4, 3, 128), f32)
    for g, r in enumerate(POOL_R):
        for v, i in enumerate((0, 1, NT - 1)):
            T = 128 * i + np.arange(128)
            cnt = np.minimum(T + r + 1, S) - np.maximum(T - r, 0)
            icnt[:, g, v, :] = (1.0 / cnt.astype(f32))[None, :]
            for ei, e in enumerate((-1, 0, 1)):
                m = (np.abs(128 * e + i_ - c_) <= r).astype(f32)
                if e == 0:
                    m = m - np.diag(cnt.astype(f32))
                band[:, g, v, ei, :] = m
    nm256 = negmask[:, 0:256]
    e0 = negmask[:, 64:192]
    c_mask = np.concatenate([nm256, nm256, e0, e0, e0, e0], axis=1).astype(f32)
    return dict(c_cs=c_cs, c_bf=c_bf.astype(f32), c_band=band.reshape(128, -1), c_icnt=icnt.reshape(128, -1),
                c_mask=c_mask)


def _get_nc():
    if "nc" not in _CACHE:
        _CACHE["nc"] = build_program()
    return _CACHE["nc"]


def make_in_maps(inputs, n_cores=8):
    f = lambda a: np.ascontiguousarray(np.asarray(a, dtype=np.float32))
    consts = _constants()
    shared = dict(
        w_in=f(inputs["w_in"][0]),
        w_pgrp=f(inputs["w_pool_grp"][0]).reshape(512, 128),
        pscale=f(np.asarray(inputs["pool_scale"][0]).reshape(4, 128).T),
        w_pbr=f(inputs["w_pool_br"][0]),
        w_abr=f(inputs["w_attn_br"][0]),
        w_out=f(inputs["w_out"][0]),
        gains=f(np.stack([np.asarray(inputs["norm_mix_pre"][0]), np.asarray(inputs["norm_mix_post"][0]),
                          np.asarray(inputs["norm_ffn_pre"][0]), np.asarray(inputs["norm_ffn_post"][0])], axis=0)),
        w_fg=f(inputs["w_ffn_gate"][0]),
        w_fu=f(inputs["w_ffn_up"][0]),
        w_fd=f(inputs["w_ffn_down"][0]),
    )
    for k_, v_ in consts.items():
        shared[k_] = f(v_)
    xs = np.asarray(inputs["x"], dtype=np.float32)
    maps = []
    for b in range(n_cores):
        m = dict(shared)
        m["x"] = np.ascontiguousarray(xs[b])
        maps.append(m)
    return maps


def kernel(**inputs):
    nc = _get_nc()
    in_maps = make_in_maps(inputs, 8)
    res = run_bass_kernel_spmd(nc, in_maps, core_ids=list(range(8)))
    out = np.stack([np.asarray(r["y"], dtype=np.float32) for r in res.results], axis=0)
    return out
```
